# Optimizing a Trainium2 kernel written in Bass

```python
import jax, jax.numpy as jnp
from jax import lax
import numpy as np

D_MODEL = 1024
BATCH = 4
SEQ = 8192
DEPTH = 2

N_HEADS = 8
N_KV_HEADS = 2
HEAD_DIM = 64
GQA_GROUP = N_HEADS // N_KV_HEADS
ATTN_WIDTH = N_HEADS * HEAD_DIM
KV_WIDTH = N_KV_HEADS * HEAD_DIM
WINDOW = 128
BLOCK = 128
ROPE_THETA = 500000.0
ROT_DIM = HEAD_DIM // 4
POOL_SIZES = (2, 4, 8, 16)
POOL_WIDTH = D_MODEL // 2
POOL_GROUP = POOL_WIDTH // len(POOL_SIZES)
N_BRANCHES = 2
IN_WIDTH = ATTN_WIDTH + 2 * KV_WIDTH + POOL_WIDTH + N_BRANCHES * D_MODEL
D_FF = 2816
EPS = 1e-6

kernel_name = "hybrid_swa_pool_macaron_encoder"


def rms_norm(x, g):
    x32 = x.astype(jnp.float32)
    y = x32 * lax.rsqrt(jnp.mean(x32 * x32, axis=-1, keepdims=True) + EPS)
    return (y * g.astype(jnp.float32)).astype(x.dtype)


def swiglu(u, w1, w3, w2):
    return (jax.nn.silu(u @ w1) * (u @ w3)) @ w2


def rope_tables(s):
    inv_freq = 1.0 / (ROPE_THETA ** (jnp.arange(0, ROT_DIM, 2, dtype=jnp.float32) / ROT_DIM))
    ang = jnp.arange(s, dtype=jnp.float32)[:, None] * inv_freq[None, :]
    return jnp.cos(ang), jnp.sin(ang)


def partial_rope(t, cos, sin):
    half = ROT_DIM // 2
    t32 = t.astype(jnp.float32)
    t1 = t32[..., :half]
    t2 = t32[..., half:ROT_DIM]
    c = cos[None, :, None, :]
    s = sin[None, :, None, :]
    rot = jnp.concatenate([t1 * c - t2 * s, t2 * c + t1 * s, t32[..., ROT_DIM:]], axis=-1)
    return rot.astype(t.dtype)


def banded_blocks(t):
    b, s = t.shape[0], t.shape[1]
    nb = s // BLOCK
    tp = jnp.pad(t, ((0, 0), (BLOCK, BLOCK), (0, 0), (0, 0)))
    parts = [tp[:, i * BLOCK:i * BLOCK + s].reshape(b, nb, BLOCK, t.shape[2], t.shape[3]) for i in range(3)]
    return jnp.concatenate(parts, axis=2)


def windowed_gqa(q, k, v, sink):
    b, s = q.shape[0], q.shape[1]
    nb = s // BLOCK
    qb = q.reshape(b, nb, BLOCK, N_KV_HEADS, GQA_GROUP, HEAD_DIM)
    kb = banded_blocks(k)
    vb = banded_blocks(v)
    scores = jnp.einsum('bnqhgd,bnkhd->bhgnqk', qb, kb).astype(jnp.float32) * (HEAD_DIM ** -0.5)
    blk = jnp.arange(nb)[:, None, None]
    qpos = blk * BLOCK + jnp.arange(BLOCK)[None, :, None]
    kpos = blk * BLOCK - BLOCK + jnp.arange(3 * BLOCK)[None, None, :]
    valid = (jnp.abs(qpos - kpos) <= WINDOW) & (kpos >= 0) & (kpos < s)
    scores = jnp.where(valid, scores, -1e30)
    sink_logit = jnp.broadcast_to(
        sink.astype(jnp.float32).reshape(1, N_KV_HEADS, GQA_GROUP, 1, 1, 1), scores.shape[:-1] + (1,))
    probs = jax.nn.softmax(jnp.concatenate([scores, sink_logit], axis=-1), axis=-1)[..., :-1]
    out = jnp.einsum('bhgnqk,bnkhd->bnqhgd', probs.astype(v.dtype), vb)
    return out.reshape(b, s, ATTN_WIDTH)


def multiscale_pool(p, pool_w, pool_scale):
    b, s, _ = p.shape
    p32 = p.astype(jnp.float32)
    cs = jnp.concatenate([jnp.zeros((b, 1, POOL_WIDTH), jnp.float32), jnp.cumsum(p32, axis=1)], axis=1)
    t = jnp.arange(s)
    outs = []
    for gi, w in enumerate(POOL_SIZES):
        sl = slice(gi * POOL_GROUP, (gi + 1) * POOL_GROUP)
        lo = jnp.clip(t - w // 2, 0, s)
        hi = jnp.clip(t + w // 2, 0, s)
        cnt = (hi - lo).astype(jnp.float32)[None, :, None]
        csg = cs[..., sl]
        mean = (jnp.take(csg, hi, axis=1) - jnp.take(csg, lo, axis=1)) / cnt
        outs.append(mean - p32[..., sl])
    pooled = jnp.stack(outs, axis=2).astype(p.dtype)
    mixed = jnp.einsum('bsgc,gcd->bsgd', pooled, pool_w).reshape(b, s, POOL_WIDTH)
    return mixed * pool_scale


def token_mixer(u, w_in, b_gate, attn_sink, pool_w, pool_scale, w_br_attn, w_br_pool, w_out, cos, sin):
    b, s, _ = u.shape
    proj = u @ w_in
    splits = np.cumsum([ATTN_WIDTH, KV_WIDTH, KV_WIDTH, POOL_WIDTH, D_MODEL]).tolist()
    q, k, v, p, ga, gb = jnp.split(proj, splits, axis=-1)
    q = partial_rope(q.reshape(b, s, N_HEADS, HEAD_DIM), cos, sin)
    k = partial_rope(k.reshape(b, s, N_KV_HEADS, HEAD_DIM), cos, sin)
    v = v.reshape(b, s, N_KV_HEADS, HEAD_DIM)
    y_attn = windowed_gqa(q, k, v, attn_sink) @ w_br_attn
    y_pool = multiscale_pool(p, pool_w, pool_scale) @ w_br_pool
    gate_a = jax.nn.sigmoid(ga + b_gate[:D_MODEL])
    gate_b = jax.nn.sigmoid(gb + b_gate[D_MODEL:])
    return (gate_a * y_attn + gate_b * y_pool) @ w_out


def _w(key, shape, fan_in):
    return jax.random.normal(key, shape, jnp.float32) * (fan_in ** -0.5)


def _gain(key, shape):
    return 1.0 + 0.05 * jax.random.normal(key, shape, jnp.float32)


def setup_inputs(seed: int = 0) -> dict:
    key = jax.random.key(seed)
    ks = jax.random.split(key, 22)
    L = DEPTH
    return {
        "x": jax.random.normal(ks[0], (BATCH, SEQ, D_MODEL), jnp.float32),
        "ffn1_norm": _gain(ks[1], (L, D_MODEL)),
        "ffn1_w1": _w(ks[2], (L, D_MODEL, D_FF), D_MODEL),
        "ffn1_w3": _w(ks[3], (L, D_MODEL, D_FF), D_MODEL),
        "ffn1_w2": _w(ks[4], (L, D_FF, D_MODEL), D_FF),
        "mix_norm": _gain(ks[5], (L, D_MODEL)),
        "w_in": _w(ks[6], (L, D_MODEL, IN_WIDTH), D_MODEL),
        "b_gate": 0.02 * jax.random.normal(ks[7], (L, N_BRANCHES * D_MODEL), jnp.float32),
        "attn_sink": 0.5 * jax.random.normal(ks[8], (L, N_HEADS), jnp.float32),
        "pool_w": _w(ks[9], (L, len(POOL_SIZES), POOL_GROUP, POOL_GROUP), POOL_GROUP),
        "pool_scale": _gain(ks[10], (L, POOL_WIDTH)),
        "w_branch_attn": _w(ks[11], (L, ATTN_WIDTH, D_MODEL), ATTN_WIDTH),
        "w_branch_pool": _w(ks[12], (L, POOL_WIDTH, D_MODEL), POOL_WIDTH),
        "w_out": _w(ks[13], (L, D_MODEL, D_MODEL), D_MODEL),
        "ffn2_norm": _gain(ks[14], (L, D_MODEL)),
        "ffn2_w1": _w(ks[15], (L, D_MODEL, D_FF), D_MODEL),
        "ffn2_w3": _w(ks[16], (L, D_MODEL, D_FF), D_MODEL),
        "ffn2_w2": _w(ks[17], (L, D_FF, D_MODEL), D_FF),
        "final_norm": _gain(ks[18], (D_MODEL,)),
    }


def reference(x, ffn1_norm, ffn1_w1, ffn1_w3, ffn1_w2, mix_norm, w_in, b_gate, attn_sink, pool_w,
              pool_scale, w_branch_attn, w_branch_pool, w_out, ffn2_norm, ffn2_w1, ffn2_w3, ffn2_w2,
              final_norm):
    cos, sin = rope_tables(x.shape[1])
    h = x
    for l in range(DEPTH):
        h = h + 0.5 * swiglu(rms_norm(h, ffn1_norm[l]), ffn1_w1[l], ffn1_w3[l], ffn1_w2[l])
        h = h + token_mixer(rms_norm(h, mix_norm[l]), w_in[l], b_gate[l], attn_sink[l], pool_w[l],
                            pool_scale[l], w_branch_attn[l], w_branch_pool[l], w_out[l], cos, sin)
        h = h + 0.5 * swiglu(rms_norm(h, ffn2_norm[l]), ffn2_w1[l], ffn2_w3[l], ffn2_w2[l])
    return rms_norm(h, final_norm)
```

```python
import os
import numpy as np
import ml_dtypes
import concourse.bass as bass
import concourse.mybir as mybir
from concourse.bass_utils import run_bass_kernel_spmd

F32 = mybir.dt.float32
BF16 = mybir.dt.bfloat16
ALU = mybir.AluOpType
AF = mybir.ActivationFunctionType
AX = mybir.AxisListType

ENGINES = ("sync", "scalar", "gpsimd", "vector", "tensor")


class Region:
    __slots__ = ("w", "r", "name", "excl")

    def __init__(self, name="", excl=False):
        self.w = None
        self.r = {}
        self.name = name
        self.excl = excl


class Sched:
    def __init__(self, nc, eng_sems):
        self.nc = nc
        self.q = {e: [] for e in ENGINES}
        self.sem = eng_sems
        self.cnt = {e: 0 for e in ENGINES}
        self.waited = {e: {} for e in ENGINES}
        self.dma_cnt = {}
        self._all_dma_sems = {}
        self.n_inst = 0

    def _collect(self, reads, writes, extra):
        deps = {}

        def add(t):
            if t is None:
                return
            k, s, v = t
            if k not in deps or deps[k][1] < v:
                deps[k] = (s, v)

        for R in reads:
            add(R.w)
            if R.excl:
                for k, (s, v) in R.r.items():
                    add((k, s, v))
        for R in writes:
            add(R.w)
            for k, (s, v) in R.r.items():
                add((k, s, v))
        for t in extra:
            add(t)
        return deps

    def _emit_waits(self, eng, deps):
        wd = self.waited[eng]
        for k, (s, v) in deps.items():
            if wd.get(k, 0) >= v:
                continue
            wd[k] = v
            self.q[eng].append(lambda e, s=s, v=v: e.wait_ge(s, v))
            self.n_inst += 1

    def _commit(self, tok, reads, writes):
        k, s, v = tok
        for R in reads:
            if R.excl:
                R.w = tok
                R.r = {}
            elif k not in R.r or R.r[k][1] < v:
                R.r[k] = (s, v)
        for R in writes:
            R.w = tok
            R.r = {}

    def op(self, eng, fn, reads=(), writes=(), extra=()):
        deps = self._collect(reads, writes, extra)
        self._emit_waits(eng, deps)
        self.cnt[eng] += 1
        s = self.sem[eng]
        tok = (eng, s, self.cnt[eng])
        self.q[eng].append(lambda e, fn=fn, s=s: fn(e).then_inc(s, 1))
        self.n_inst += 1
        self._commit(tok, reads, writes)
        return tok

    def group(self, eng, fns, reads=(), writes=(), extra=()):
        deps = self._collect(reads, writes, extra)
        self._emit_waits(eng, deps)
        self.cnt[eng] += 1
        s = self.sem[eng]
        tok = (eng, s, self.cnt[eng])
        for fn in fns[:-1]:
            self.q[eng].append(lambda e, fn=fn: fn(e))
        last = fns[-1]
        self.q[eng].append(lambda e, fn=last, s=s: fn(e).then_inc(s, 1))
        self.n_inst += len(fns)
        self._commit(tok, reads, writes)
        return tok

    def dma(self, eng, sem, out, in_, reads=(), writes=(), extra=(), n=1, fn=None):
        deps = self._collect(reads, writes, extra)
        self._emit_waits(eng, deps)
        key = ("dma", id(sem))
        self._all_dma_sems[key] = sem
        self.dma_cnt[key] = self.dma_cnt.get(key, 0) + 16
        tok = (key, sem, self.dma_cnt[key])
        if fn is None:
            self.q[eng].append(lambda e, o=out, i=in_, s=sem: e.dma_start(out=o, in_=i).then_inc(s, 16))
        else:
            self.q[eng].append(lambda e, fn=fn, s=sem: fn(e).then_inc(s, 16))
        self.n_inst += 1
        self._commit(tok, reads, writes)
        return tok

    def wait_all(self, eng, toks):
        deps = {}
        for t in toks:
            if t is None:
                continue
            k, s, v = t
            if k not in deps or deps[k][1] < v:
                deps[k] = (s, v)
        self._emit_waits(eng, deps)

    def run(self, block):
        q = self.q

        @block.sync
        def _(e):
            for f in q["sync"]:
                f(e)

        @block.scalar
        def _(e):
            for f in q["scalar"]:
                f(e)

        @block.gpsimd
        def _(e):
            for f in q["gpsimd"]:
                f(e)

        @block.vector
        def _(e):
            for f in q["vector"]:
                f(e)

        @block.tensor
        def _(e):
            for f in q["tensor"]:
                f(e)

    def dma_multi(self, eng, sem, pairs, reads=(), writes=(), extra=()):
        deps = self._collect(reads, writes, extra)
        self._emit_waits(eng, deps)
        key = ("dma", id(sem))
        self._all_dma_sems[key] = sem
        for (o, i) in pairs:
            self.dma_cnt[key] = self.dma_cnt.get(key, 0) + 16
            self.q[eng].append(lambda e, o=o, i=i, s=sem: e.dma_start(out=o, in_=i).then_inc(s, 16))
            self.n_inst += 1
        tok = (key, sem, self.dma_cnt[key])
        self._commit(tok, reads, writes)
        return tok

    def barrier(self):
        toks = []
        for e in ENGINES:
            if self.cnt[e] > 0:
                toks.append((e, self.sem[e], self.cnt[e]))
        for key, v in self.dma_cnt.items():
            toks.append((key, self._all_dma_sems[key], v))
        for e in ENGINES:
            self.wait_all(e, toks)


D = 1024
DFF = 2816
NCH = D // 128
NJ = DFF // 128
GROUPS = [(0, 4), (4, 4), (8, 4), (12, 4), (16, 3), (19, 3)]
DEPTH = 2
NHEADS = 8
FB = 12
MB = 4
EPS = 1e-6
ROPE_THETA = 500000.0

WSHAPES = {
    "ffn1_norm": [DEPTH, D], "ffn1_w1": [DEPTH, D, DFF], "ffn1_w3": [DEPTH, D, DFF], "ffn1_w2": [DEPTH, DFF, D],
    "mix_norm": [DEPTH, D], "w_in": [DEPTH, D, 3328], "pool_w": [DEPTH, 4, 128, 128],
    "w_branch_attn": [DEPTH, 512, D], "w_branch_pool": [DEPTH, 512, D], "w_out": [DEPTH, D, D],
    "ffn2_norm": [DEPTH, D], "ffn2_w1": [DEPTH, D, DFF], "ffn2_w3": [DEPTH, D, DFF], "ffn2_w2": [DEPTH, DFF, D],
    "final_norm": [D],
    "b_gate_t": [DEPTH, 128, 16], "pool_scale_t": [DEPTH, 128, 4], "sink_b": [DEPTH, 128, 8],
}


def split_blocks(lo, hi, maxb):
    n = hi - lo
    nb = -(-n // maxb)
    base, rem = n // nb, n % nb
    out, s = [], lo
    for i in range(nb):
        sz = base + (1 if i < rem else 0)
        out.append((s, s + sz))
        s += sz
    return out


def subs_of(T):
    return [(a, min(a + 512, T)) for a in range(0, T, 512)]


class WStream:
    def __init__(self, S, slots, regs, sems):
        self.S, self.slots, self.regs, self.sems = S, slots, regs, sems
        self.items = []
        self.kidx = []
        self.count = {"A": 0, "B": 0}
        self.emitted = 0
        self.next = 0
        self.released = 0

    def add(self, kind, pairs_fn):
        self.items.append((kind, pairs_fn))
        self.kidx.append(self.count[kind])
        self.count[kind] += 1

    def _prev_occupant(self, j):
        kind, _ = self.items[j]
        nslots = len(self.slots[kind])
        want = self.kidx[j] - nslots
        if want < 0:
            return -1
        for p in range(j - 1, -1, -1):
            if self.items[p][0] == kind and self.kidx[p] == want:
                return p
        return -1

    def _emit(self, j):
        kind, pairs_fn = self.items[j]
        s = self.kidx[j] % len(self.slots[kind])
        self.S.dma_multi("gpsimd", self.sems[kind][s], pairs_fn(self.slots[kind][s]), writes=[self.regs[kind][s]])

    def get(self, kind):
        i = self.next
        assert self.items[i][0] == kind, (i, self.items[i][0], kind)
        self.next += 1
        while self.emitted <= i:
            assert self._prev_occupant(self.emitted) < self.released, "slot still in use"
            self._emit(self.emitted)
            self.emitted += 1
        s = self.kidx[i] % len(self.slots[kind])
        return self.slots[kind][s], self.regs[kind][s]

    def release(self):
        self.released = self.next
        while self.emitted < len(self.items) and self._prev_occupant(self.emitted) < self.released:
            self._emit(self.emitted)
            self.emitted += 1


def build_program(NH, stop_after=None):
    from contextlib import ExitStack
    nc = bass.Bass("TRN2", target_bir_lowering=False)
    NT = NH + 4

    def din(name, shape, dt=F32):
        return nc.dram_tensor(name, list(shape), dt, kind="ExternalInput").ap()

    x_pad = din("x_pad", [NT * 128, D])
    W = {n: din(n, shp) for n, shp in WSHAPES.items()}
    ident_d = din("ident", [128, 128], BF16)
    rmat_d = din("rmat", [64, 64], BF16)
    masks_d = din("masks", [4, 128, 128], BF16)
    ropeC_d = din("ropeC", [64, NT * 128])
    ropeS_d = din("ropeS", [64, NT * 128])
    pfix_d = din("pfix", [128, 4, 16])
    valid_d = din("valid", [128, 2])
    out_d = nc.dram_tensor("out", [NH * 128, D], F32, kind="ExternalOutput").ap()
    Hb = [nc.dram_tensor("hres%d" % i, [NT * 128, D], F32, kind="Internal").ap() for i in range(2)]

    with ExitStack() as es:
        E = es.enter_context

        def sb(name, shape, dt):
            return E(nc.sbuf_tensor(name, list(shape), dt))

        sems = {e: E(nc.semaphore("s_" + e)) for e in ENGINES}
        S = Sched(nc, sems)
        _semn = [0]

        def newsem():
            _semn[0] += 1
            return E(nc.semaphore("d%d" % _semn[0]))

        slotA = [sb("slotA%d" % i, [128, 8192], BF16) for i in range(2)]
        slotB = [sb("slotB%d" % i, [128, 4096], BF16) for i in range(2)]
        RA = [Region("A0"), Region("A1")]
        RB = [Region("B0"), Region("B1")]
        WS = WStream(S, {"A": slotA, "B": slotB}, {"A": RA, "B": RB},
                     {"A": [newsem(), newsem()], "B": [newsem(), newsem()]})
        uT = sb("uT", [128, NCH, FB * 128], BF16)
        RuT = [Region("uT%d" % i) for i in range(FB)]
        gt = sb("gt", [128, D], F32); Rgt = Region("gt"); sem_g = newsem()
        gt2 = sb("gt2", [128, D], F32); Rgt2 = Region("gt2")
        utmp = [sb("utmp%d" % i, [128, D], BF16) for i in range(4)]; Rutmp = [Region() for _ in range(4)]
        st = sb("st", [128, 4, 4], F32); Rst = [Region() for _ in range(4)]
        ident = sb("ident_s", [128, 128], BF16); Rconst = Region("const")
        rmat = sb("rmat_s", [64, 64], BF16)
        masks = sb("masks_s", [128, 4, 128], BF16)
        pfix = sb("pfix_s", [128, 4, 16], F32)
        valid = sb("valid_s", [128, 2], F32)
        bg_all = sb("bg_s", [128, DEPTH, 16], F32)
        psc_all = sb("psc_s", [128, DEPTH, 4], F32)
        esink = sb("esink_s", [128, DEPTH, 8], F32)
        PS = [E(nc.psum_tensor("ps%d" % i, [128, 1024], F32)) for i in range(4)]
        Rb = [Region("bank%d" % i, excl=True) for i in range(8)]

        def bank(k):
            return PS[k // 2][:, (k % 2) * 512:(k % 2) * 512 + 512]

        RH = [[Region("hres%d_%d" % (b, i)) for i in range(NT)] for b in range(2)]
        xsem_g = [newsem() for _ in range(FB)]
        ssem_g = [newsem() for _ in range(FB)]
        sem_g2 = newsem()
        rsem_g = newsem()
        block = E(nc.Block())

        sem_c = newsem()
        S.dma_multi("sync", sem_c, [
            (ident[:], ident_d), (rmat[:], rmat_d), (masks[:], masks_d.rearrange("m p q -> p m q")),
            (pfix[:], pfix_d), (valid[:], valid_d),
            (bg_all[:], W["b_gate_t"].rearrange("l p m -> p l m")),
            (psc_all[:], W["pool_scale_t"].rearrange("l p m -> p l m")),
            (esink[:], W["sink_b"].rearrange("l p m -> p l m")),
        ], writes=[Rconst])
        S.op("scalar", lambda e: e.activation(out=esink[:], in_=esink[:], func=AF.Exp), reads=[Rconst], writes=[Rconst])

        def gload(dst, Rd, name, off, sem=None):
            src = bass.AP(W[name].tensor, off, [[0, 128], [1, D]])
            S.dma("sync", sem_g if sem is None else sem, dst[:], src, writes=[Rd])

        def ffn_items(l, which):
            pre = "ffn%d" % which
            w1, w3, w2 = W[pre + "_w1"][l], W[pre + "_w3"][l], W[pre + "_w2"][l]
            for (j0, ng) in GROUPS:
                def pa(slot, j0=j0, ng=ng, w1=w1, w3=w3):
                    v = slot[:, :].rearrange("p (a c n) -> p a c n", a=2, c=NCH)
                    return [(v[:, 0, :, 0:ng * 128], w1[:, j0 * 128:(j0 + ng) * 128].rearrange("(c p) n -> p c n", p=128)),
                            (v[:, 1, :, 0:ng * 128], w3[:, j0 * 128:(j0 + ng) * 128].rearrange("(c p) n -> p c n", p=128))]

                def pb(slot, j0=j0, ng=ng, w2=w2):
                    v = slot[:, :].rearrange("p (j n) -> p j n", j=4)
                    return [(v[:, 0:ng, :], w2[j0 * 128:(j0 + ng) * 128, :].rearrange("(j p) n -> p j n", p=128))]
                WS.add("A", pa)
                WS.add("B", pb)

        def mixer_items(l):
            win = W["w_in"][l]

            def a_qkv(slot):
                v = slot[:, 0:6144].rearrange("p (c n) -> p c n", c=NCH)
                pw = slot[:, 6144:6656].rearrange("p (g d) -> p g d", g=4)
                return [(v, win[:, 0:768].rearrange("(c p) n -> p c n", p=128)),
                        (pw, W["pool_w"][l].rearrange("g c d -> c g d"))]

            def b_p(slot):
                return [(slot[:, :].rearrange("p (c n) -> p c n", c=NCH), win[:, 768:1280].rearrange("(c p) n -> p c n", p=128))]

            def b_bra(slot):
                return [(slot[:, :].rearrange("p (c n) -> p c n", c=4), W["w_branch_attn"][l].rearrange("(c p) n -> p c n", p=128))]

            def a_ga(slot):
                return [(slot[:, :].rearrange("p (c n) -> p c n", c=NCH), win[:, 1280:2304].rearrange("(c p) n -> p c n", p=128))]

            def b_brp(slot):
                return [(slot[:, :].rearrange("p (c n) -> p c n", c=4), W["w_branch_pool"][l].rearrange("(c p) n -> p c n", p=128))]

            def a_gb(slot):
                return [(slot[:, :].rearrange("p (c n) -> p c n", c=NCH), win[:, 2304:3328].rearrange("(c p) n -> p c n", p=128))]

            def a_out(slot):
                return [(slot[:, :].rearrange("p (c n) -> p c n", c=NCH), W["w_out"][l].rearrange("(c p) n -> p c n", p=128))]
            for kind, fn in (("A", a_qkv), ("B", b_p), ("B", b_bra), ("A", a_ga), ("B", b_brp), ("A", a_gb), ("A", a_out)):
                WS.add(kind, fn)

        phases = []
        for l in range(DEPTH):
            h_in = DEPTH - l
            phases.append(("ffn", l, 1, -h_in, NH + h_in))
            phases.append(("mix", l, 0, -(h_in - 1), NH + h_in - 1))
            phases.append(("ffn", l, 2, -(h_in - 1), NH + h_in - 1))
        if stop_after is not None:
            phases = phases[:stop_after]
        for (kind, l, which, lo, hi) in phases:
            if kind == "ffn":
                for _ in split_blocks(lo, hi, FB):
                    ffn_items(l, which)
            else:
                for _ in split_blocks(lo, hi, MB):
                    mixer_items(l)

        cnt = {"n": 0, "t": 0}

        def norm_tile(xap, Rx, g, Rg, col0):
            k = cnt["n"] % 4
            cnt["n"] += 1
            S.op("scalar", lambda e: e.activation(out=utmp[k][:], in_=xap, func=AF.Square, accum_out=st[:, k, 0:1]),
                 reads=[Rx], writes=[Rutmp[k], Rst[k]])
            S.op("vector", lambda e: e.tensor_scalar(out=st[:, k, 1:2], in0=st[:, k, 0:1], scalar1=1.0 / D, scalar2=EPS,
                                                      op0=ALU.mult, op1=ALU.add), reads=[Rst[k]], writes=[Rst[k]])
            S.op("scalar", lambda e: e.activation(out=st[:, k, 2:3], in_=st[:, k, 1:2], func=AF.Sqrt),
                 reads=[Rst[k]], writes=[Rst[k]])
            S.op("vector", lambda e: e.reciprocal(out=st[:, k, 3:4], in_=st[:, k, 2:3]), reads=[Rst[k]], writes=[Rst[k]])
            S.op("vector", lambda e: e.scalar_tensor_tensor(out=utmp[k][:], in0=xap, scalar=st[:, k, 3:4], in1=g[:],
                                                             op0=ALU.mult, op1=ALU.mult),
                 reads=[Rx, Rst[k], Rg], writes=[Rutmp[k]])
            bk = (4, 6, 5, 7)[cnt["t"] % 4]
            cnt["t"] += 1
            pT = bank(bk).bitcast(BF16)
            S.group("tensor", [
                (lambda e, c=c: e.transpose(out=pT[:, c * 128:(c + 1) * 128], in_=utmp[k][:, c * 128:(c + 1) * 128],
                                            identity=ident[:])) for c in range(NCH)],
                reads=[Rutmp[k], Rconst], writes=[Rb[bk]])
            ti = col0 // 128
            S.op("scalar", lambda e: e.copy(out=uT[:, :, col0:col0 + 128], in_=pT.rearrange("p (c t) -> p c t", c=NCH)),
                 reads=[Rb[bk]], writes=[RuT[ti]])
            return k

        def uT_regs(n0, n1):
            return [RuT[i] for i in range(n0 // 128, (n1 + 127) // 128)]

        def ffn_phase(l, which, lo, hi, first, final, hi_):
            pre = "ffn%d" % which
            hres = Hb[hi_]
            Rhres = RH[hi_]
            src = x_pad if first else hres
            with ExitStack() as ps:
                uid = "_f%d%d" % (l, which)
                xblk = ps.enter_context(nc.sbuf_tensor("xblk" + uid, [128, FB, D], F32))
                hbuf = [ps.enter_context(nc.sbuf_tensor("hb%d%s" % (i, uid), [128, 4, FB * 128], BF16)) for i in range(2)]
                silu = [ps.enter_context(nc.sbuf_tensor("silu%d%s" % (i, uid), [128, 512], F32)) for i in range(2)]
                Rx = [Region() for _ in range(FB)]
                Rh = [Region(), Region()]
                Rsilu = [Region(), Region()]
                xsem = xsem_g
                ssem = ssem_g
                gload(gt, Rgt, pre + "_norm", l * D)
                if final:
                    gload(gt2, Rgt2, "final_norm", 0, sem=sem_g2)
                cf = {"gc": 0, "uc": 0, "dc": 0}
                out_toks = []

                def do_block(b0, b1):
                    gc, uc, dc = cf["gc"], cf["uc"], cf["dc"]
                    nt = b1 - b0
                    T = nt * 128
                    for i in range(nt):
                        t = b0 + i
                        ti = t + 2
                        S.dma("sync", xsem[i], xblk[:, i, :], src[ti * 128:(ti + 1) * 128, :],
                              reads=([] if first else [Rhres[ti]]), writes=[Rx[i]])
                        if l >= 1 and which == 1 and (t < 0 or t >= NH):
                            col = 0 if t < 0 else 1
                            S.op("vector", lambda e, i=i, col=col: e.tensor_scalar(
                                out=xblk[:, i, :], in0=xblk[:, i, :], scalar1=valid[:, col:col + 1], scalar2=None,
                                op0=ALU.mult), reads=[Rx[i], Rconst], writes=[Rx[i]])
                        norm_tile(xblk[:, i, :], Rx[i], gt, Rgt, i * 128)
                    subs = subs_of(T)
                    for (j0, ng) in GROUPS:
                        sa, Rsa = WS.get("A")
                        sbw, Rsb = WS.get("B")
                        wa = sa[:, :].rearrange("p (a c n) -> p a c n", a=2, c=NCH)
                        wb = sbw[:, :].rearrange("p (j n) -> p j n", j=4)
                        hb, Rhb = hbuf[gc % 2], Rh[gc % 2]
                        gc += 1
                        for jj in range(ng):
                            for (n0, n1) in subs:
                                n = n1 - n0
                                ba, bb = (0, 1) if uc % 2 == 0 else (2, 3)
                                sl, Rsl = silu[uc % 2], Rsilu[uc % 2]
                                uc += 1
                                for (bk, a) in ((ba, 0), (bb, 1)):
                                    S.group("tensor", [
                                        (lambda e, c=c, bk=bk, a=a, jj=jj, n0=n0, n1=n1, n=n, wa=wa: e.matmul(
                                            bank(bk)[:, 0:n], lhsT=wa[:, a, c, jj * 128:(jj + 1) * 128], rhs=uT[:, c, n0:n1],
                                            start=(c == 0), stop=(c == NCH - 1))) for c in range(NCH)],
                                        reads=[Rsa] + uT_regs(n0, n1), writes=[Rb[bk]])
                                S.op("scalar", lambda e, ba=ba, sl=sl, n=n: e.activation(out=sl[:, 0:n], in_=bank(ba)[:, 0:n], func=AF.Silu),
                                     reads=[Rb[ba]], writes=[Rsl])
                                S.op("vector", lambda e, bb=bb, sl=sl, n=n, hb=hb, jj=jj, n0=n0, n1=n1: e.tensor_tensor(
                                    out=hb[:, jj, n0:n1], in0=sl[:, 0:n], in1=bank(bb)[:, 0:n], op=ALU.mult),
                                    reads=[Rsl, Rb[bb]], writes=[Rhb])
                        for i in range(nt):
                            pd = 2 + dc % 2
                            dc += 1
                            S.group("tensor", [
                                (lambda e, pd=pd, half=half, jj=jj, i=i, hb=hb, wb=wb, ng=ng: e.matmul(
                                    PS[pd][:, half * 512:(half + 1) * 512], lhsT=hb[:, jj, i * 128:(i + 1) * 128],
                                    rhs=wb[:, jj, half * 512:(half + 1) * 512], start=(jj == 0), stop=(jj == ng - 1)))
                                for half in range(2) for jj in range(ng)],
                                reads=[Rhb, Rsb], writes=[Rb[2 * pd], Rb[2 * pd + 1]])
                            for half in range(2):
                                S.op("vector", lambda e, pd=pd, half=half, i=i: e.scalar_tensor_tensor(
                                    out=xblk[:, i, half * 512:(half + 1) * 512], in0=PS[pd][:, half * 512:(half + 1) * 512],
                                    scalar=0.5, in1=xblk[:, i, half * 512:(half + 1) * 512], op0=ALU.mult, op1=ALU.add),
                                    reads=[Rb[2 * pd + half], Rx[i]], writes=[Rx[i]])
                        WS.release()
                    for i in range(nt):
                        t = b0 + i
                        ti = t + 2
                        if not final:
                            S.dma("sync", ssem[i], hres[ti * 128:(ti + 1) * 128, :], xblk[:, i, :],
                                  reads=[Rx[i]], writes=[Rhres[ti]])
                        else:
                            k = cnt["n"] % 4
                            cnt["n"] += 1
                            xap = xblk[:, i, :]
                            S.op("scalar", lambda e, xap=xap, k=k: e.activation(out=utmp[k][:], in_=xap, func=AF.Square, accum_out=st[:, k, 0:1]),
                                 reads=[Rx[i]], writes=[Rutmp[k], Rst[k]])
                            S.op("vector", lambda e, k=k: e.tensor_scalar(out=st[:, k, 1:2], in0=st[:, k, 0:1], scalar1=1.0 / D, scalar2=EPS,
                                                                          op0=ALU.mult, op1=ALU.add), reads=[Rst[k]], writes=[Rst[k]])
                            S.op("scalar", lambda e, k=k: e.activation(out=st[:, k, 2:3], in_=st[:, k, 1:2], func=AF.Sqrt),
                                 reads=[Rst[k]], writes=[Rst[k]])
                            S.op("vector", lambda e, k=k: e.reciprocal(out=st[:, k, 3:4], in_=st[:, k, 2:3]), reads=[Rst[k]], writes=[Rst[k]])
                            S.op("vector", lambda e, xap=xap, k=k: e.scalar_tensor_tensor(out=xap, in0=xap, scalar=st[:, k, 3:4], in1=gt2[:],
                                                                                         op0=ALU.mult, op1=ALU.mult),
                                 reads=[Rx[i], Rst[k], Rgt2], writes=[Rx[i]])
                            out_toks.append(S.dma("sync", ssem[i], out_d[t * 128:(t + 1) * 128, :], xap, reads=[Rx[i]]))
                    cf["gc"], cf["uc"], cf["dc"] = gc, uc, dc

                for (b0, b1) in split_blocks(lo, hi, FB):
                    do_block(b0, b1)
                S.barrier()
                return out_toks

        def bcast_last(ap, n):
            return bass.AP(ap.tensor, ap.offset, [list(d) for d in ap.ap] + [[0, n]])

        def bcast_mid(ap, n):
            d = [list(x) for x in ap.ap]
            return bass.AP(ap.tensor, ap.offset, [d[0], [0, n]] + d[1:])

        def mixer_phase(l, lo, hi, hsrc):
            hres, Rhres = Hb[hsrc], RH[hsrc]
            hdst, Rhdst = Hb[1 - hsrc], RH[1 - hsrc]
            with ExitStack() as ps:
                def pb_(name, shape, dt):
                    return ps.enter_context(nc.sbuf_tensor(name + "_m%d" % l, list(shape), dt))
                ME = MB + 2
                xblk = pb_("xblk_m", [128, MB, D], F32)
                xh = pb_("xh", [128, 2, D], F32)
                qT = pb_("qT", [64, MB, NHEADS, 128], BF16)
                kT = pb_("kT", [64, 2, ME * 128], BF16)
                vaug = pb_("vaug", [128, ME, 2, 80], BF16)
                rC = pb_("rC", [64, ME * 128], F32)
                rS = pb_("rS", [64, ME * 128], F32)
                Pb2 = [pb_("Pb%d" % i, [128, 2, 3, 512], BF16) for i in range(2)]
                Osb = pb_("Osb", [128, 512], BF16)
                OT = pb_("OT", [128, 4, MB * 128], BF16)
                za = pb_("za", [128, NCH, MB * 128], F32)
                zT = pb_("zT", [128, NCH, MB * 128], BF16)
                PW = ME * 128 + 32
                pbuf = pb_("pbuf", [128, PW], F32)
                sa = pb_("sa", [128, PW], F32)
                sbb = pb_("sbb", [128, PW], F32)
                pooled = pb_("pooled", [128, 4, MB * 128], BF16)
                mixed = pb_("mixed", [128, 4, MB * 128], BF16)
                tq2 = [pb_("tq%d" % i, [64, 512], BF16) for i in range(2)]
                t12 = [pb_("t1%d" % i, [64, 512], F32) for i in range(2)]
                t22 = [pb_("t2%d" % i, [64, 512], F32) for i in range(2)]
                sig = [pb_("sig%d" % i, [128, 512], F32) for i in range(2)]
                tg = pb_("tg", [128, 512], F32)
                dsum = pb_("dsum", [128, 8], F32)
                rinv = pb_("rinv", [128, 8], F32)
                tmp8 = pb_("tmp8", [128, 8], F32)
                Rx = [Region() for _ in range(MB)]
                Rxh = [Region(), Region()]
                RqT, RkT, Rv, Rrope = Region(), Region(), Region(), Region()
                RP2 = [[Region(), Region()], [Region(), Region()]]
                RO, ROT, Rza, RzT = Region(), Region(), Region(), Region()
                Rp, Rsa, Rsb, Rpooled, Rmixed = Region(), Region(), Region(), Region(), Region()
                Rtg, Rds, Rt8 = Region(), Region(), Region()
                Rtq2, Rt12, Rt22 = [Region(), Region()], [Region(), Region()], [Region(), Region()]
                Rsig = [Region(), Region()]
                xsem = xsem_g
                ssem = ssem_g
                rsem = rsem_g
                gload(gt, Rgt, "mix_norm", l * D)
                S.op("vector", lambda e: e.memset(vaug[:].rearrange("p a b c -> p (a b c)"), 1.0), writes=[Rv])
                S.op("vector", lambda e: e.memset(pbuf[:], 0.0), writes=[Rp])
                c2 = {"q": 0, "r": 0, "v": 0, "s": 0, "po": 0, "y": 0, "g": 0, "d": 0}

                def rope(bq, n, c0, dst, Rdst):
                    kk = c2["r"] % 2
                    tq, t1, t2 = tq2[kk], t12[kk], t22[kk]
                    Rtq, Rt1, Rt2 = Rtq2[kk], Rt12[kk], Rt22[kk]
                    S.op("scalar", lambda e: e.copy(out=tq[:, 0:n], in_=bank(bq)[0:64, 0:n]), reads=[Rb[bq]], writes=[Rtq])
                    br = 2 + c2["r"] % 2
                    c2["r"] += 1
                    S.group("tensor", [lambda e: e.matmul(bank(br)[0:64, 0:n], lhsT=rmat[:, :], rhs=tq[:, 0:n], start=True, stop=True)],
                            reads=[Rtq, Rconst], writes=[Rb[br]])
                    S.op("vector", lambda e: e.tensor_tensor(out=t1[:, 0:n], in0=bank(bq)[0:64, 0:n], in1=rC[:, c0:c0 + n], op=ALU.mult),
                         reads=[Rb[bq], Rrope], writes=[Rt1])
                    S.op("vector", lambda e: e.tensor_tensor(out=t2[:, 0:n], in0=bank(br)[0:64, 0:n], in1=rS[:, c0:c0 + n], op=ALU.mult),
                         reads=[Rb[br], Rrope], writes=[Rt2])
                    if len(dst.shape) == 3:
                        i0 = t1[:, 0:n].rearrange("p (a b) -> p a b", b=128)
                        i1 = t2[:, 0:n].rearrange("p (a b) -> p a b", b=128)
                    else:
                        i0, i1 = t1[:, 0:n], t2[:, 0:n]
                    S.op("vector", lambda e: e.tensor_tensor(out=dst, in0=i0, in1=i1, op=ALU.add),
                         reads=[Rt1, Rt2], writes=[Rdst])

                def do_block(m0, m1):
                    nt = m1 - m0
                    TO = nt * 128
                    TE = (nt + 2) * 128
                    for e_ in range(nt + 2):
                        ti = m0 - 1 + e_ + 2
                        if e_ == 0:
                            dst, Rd = xh[:, 0, :], Rxh[0]
                        elif e_ == nt + 1:
                            dst, Rd = xh[:, 1, :], Rxh[1]
                        else:
                            dst, Rd = xblk[:, e_ - 1, :], Rx[e_ - 1]
                        S.dma("sync", xsem[e_], dst, hres[ti * 128:(ti + 1) * 128, :], reads=[Rhres[ti]], writes=[Rd])
                    pc0 = (m0 + 1) * 128
                    S.dma_multi("sync", rsem, [(rC[:, 0:TE], ropeC_d[:, pc0:pc0 + TE]), (rS[:, 0:TE], ropeS_d[:, pc0:pc0 + TE])],
                                writes=[Rrope])
                    for e_ in range(nt + 2):
                        if e_ == 0:
                            xap, Rd = xh[:, 0, :], Rxh[0]
                        elif e_ == nt + 1:
                            xap, Rd = xh[:, 1, :], Rxh[1]
                        else:
                            xap, Rd = xblk[:, e_ - 1, :], Rx[e_ - 1]
                        norm_tile(xap, Rd, gt, Rgt, e_ * 128)
                    MS = 9
                    sq_, Rsq = WS.get("A")
                    if MS < 2:
                        WS.release()
                        for _ in range(6):
                            WS.get(WS.items[WS.next][0])
                            WS.release()
                        return
                    wq = sq_[:, 0:6144].rearrange("p (c n) -> p c n", c=NCH)
                    poolw = sq_[:, 6144:6656].rearrange("p (g d) -> p g d", g=4)
                    for kvh in range(2):
                        for (n0, n1) in subs_of(TE):
                            n = n1 - n0
                            bq = c2["q"] % 2
                            c2["q"] += 1
                            S.group("tensor", [
                                (lambda e, c=c, bq=bq, n=n, n0=n0, n1=n1, kvh=kvh: e.matmul(
                                    bank(bq)[0:64, 0:n], lhsT=wq[:, c, 512 + 64 * kvh:512 + 64 * kvh + 64], rhs=uT[:, c, n0:n1],
                                    start=(c == 0), stop=(c == NCH - 1))) for c in range(NCH)],
                                reads=[Rsq] + uT_regs(n0, n1), writes=[Rb[bq]])
                            rope(bq, n, n0, kT[:, kvh, n0:n1], RkT)
                    for h in range(NHEADS):
                        for (n0, n1) in subs_of(TO):
                            n = n1 - n0
                            bq = c2["q"] % 2
                            c2["q"] += 1
                            S.group("tensor", [
                                (lambda e, c=c, bq=bq, n=n, n0=n0, n1=n1, h=h: e.matmul(
                                    bank(bq)[0:64, 0:n], lhsT=wq[:, c, 64 * h:64 * h + 64], rhs=uT[:, c, 128 + n0:128 + n1],
                                    start=(c == 0), stop=(c == NCH - 1))) for c in range(NCH)],
                                reads=[Rsq] + uT_regs(128 + n0, 128 + n1), writes=[Rb[bq]])
                            rope(bq, n, 128 + n0, qT[:, n0 // 128:n1 // 128, h, :], RqT)
                    for e_ in range(nt + 2):
                        bv = 5 + 2 * (c2["v"] % 2)
                        c2["v"] += 1
                        S.group("tensor", [
                            (lambda e, c=c, bv=bv, e_=e_: e.matmul(bank(bv)[:, 0:128], lhsT=uT[:, c, e_ * 128:(e_ + 1) * 128],
                                                                   rhs=wq[:, c, 640:768], start=(c == 0), stop=(c == NCH - 1)))
                            for c in range(NCH)], reads=[Rsq, RuT[e_]], writes=[Rb[bv]])
                        S.op("scalar", lambda e, bv=bv, e_=e_: e.copy(out=vaug[:, e_, :, 0:64],
                                                                      in_=bank(bv)[:, 0:128].rearrange("p (h d) -> p h d", h=2)),
                             reads=[Rb[bv]], writes=[Rv])
                    sp_, Rsp = WS.get("B")
                    if MS < 3:
                        WS.release()
                        for _ in range(5):
                            WS.get(WS.items[WS.next][0])
                            WS.release()
                        return
                    wp = sp_[:, :].rearrange("p (c n) -> p c n", c=NCH)
                    for g in range(4):
                        for (n0, n1) in subs_of(TE):
                            n = n1 - n0
                            bq = c2["q"] % 2
                            c2["q"] += 1
                            S.group("tensor", [
                                (lambda e, c=c, bq=bq, n=n, n0=n0, n1=n1, g=g: e.matmul(
                                    bank(bq)[:, 0:n], lhsT=wp[:, c, g * 128:(g + 1) * 128], rhs=uT[:, c, n0:n1],
                                    start=(c == 0), stop=(c == NCH - 1))) for c in range(NCH)],
                                reads=[Rsp] + uT_regs(n0, n1), writes=[Rb[bq]])
                            S.op("scalar", lambda e, bq=bq, n=n, n0=n0, n1=n1: e.copy(out=pbuf[:, 16 + n0:16 + n1], in_=bank(bq)[:, 0:n]),
                                 reads=[Rb[bq]], writes=[Rp])
                        K = g + 1
                        w = 1 << K
                        hw = w // 2
                        A = 16 + 128 - hw
                        B = A + TO
                        src, Rsrc = pbuf, Rp
                        for k in range(1, K + 1):
                            hi_ = B + sum(1 << i for i in range(k, K))
                            dstb, Rdst = (sa, Rsa) if k % 2 == 1 else (sbb, Rsb)
                            sh = 1 << (k - 1)
                            S.op("vector", lambda e, src=src, dstb=dstb, sh=sh, hi_=hi_, A=A: e.tensor_tensor(
                                out=dstb[:, A:hi_], in0=src[:, A:hi_], in1=src[:, A + sh:hi_ + sh], op=ALU.add),
                                reads=[Rsrc], writes=[Rdst])
                            src, Rsrc = dstb, Rdst
                        S.op("vector", lambda e, src=src, g=g, w=w, A=A, B=B: e.scalar_tensor_tensor(
                            out=pooled[:, g, 0:TO], in0=src[:, A:B], scalar=1.0 / w, in1=pbuf[:, 144:144 + TO],
                            op0=ALU.mult, op1=ALU.subtract), reads=[Rsrc, Rp], writes=[Rpooled])
                        edges = []
                        if m0 <= 0 < m1:
                            edges.append(((0 - m0) * 128, 0))
                        if m0 <= NH - 1 < m1:
                            edges.append(((NH - 1 - m0) * 128 + 120, 8))
                        for (c0, f0) in edges:
                            S.op("vector", lambda e, src=src, g=g, A=A, c0=c0, f0=f0: e.tensor_tensor(
                                out=tmp8[:, :], in0=src[:, A + c0:A + c0 + 8], in1=pfix[:, g, f0:f0 + 8], op=ALU.mult),
                                reads=[Rsrc, Rconst], writes=[Rt8])
                            S.op("vector", lambda e, g=g, c0=c0: e.tensor_tensor(
                                out=pooled[:, g, c0:c0 + 8], in0=tmp8[:, :], in1=pbuf[:, 144 + c0:144 + c0 + 8], op=ALU.subtract),
                                reads=[Rt8, Rp], writes=[Rpooled])
                    for g in range(4):
                        bm = 2 + c2["r"] % 2
                        c2["r"] += 1
                        S.group("tensor", [lambda e, bm=bm, g=g: e.matmul(bank(bm)[:, 0:TO], lhsT=poolw[:, g, :], rhs=pooled[:, g, 0:TO],
                                                                         start=True, stop=True)],
                                reads=[Rsq, Rpooled], writes=[Rb[bm]])
                        S.op("vector", lambda e, bm=bm, g=g: e.tensor_scalar(out=mixed[:, g, 0:TO], in0=bank(bm)[:, 0:TO],
                                                                            scalar1=psc_all[:, l, g:g + 1], scalar2=None, op0=ALU.mult),
                             reads=[Rb[bm], Rconst], writes=[Rmixed])
                    WS.release()
                    if MS < 4:
                        for _ in range(5):
                            WS.get(WS.items[WS.next][0])
                            WS.release()
                        return
                    for i in range(nt):
                        e_ = i + 1
                        t = m0 + i
                        pp = 2 + c2["po"] % 2
                        Pb, RP = Pb2[c2["po"] % 2], RP2[c2["po"] % 2]
                        c2["po"] += 1
                        for kvh in range(2):
                            for kc in range(3):
                                ke = e_ - 1 + kc
                                bs = c2["s"] % 3
                                c2["s"] += 1
                                S.group("tensor", [lambda e, bs=bs, kvh=kvh, ke=ke, i=i: e.matmul(
                                    bank(bs)[:, 0:512], lhsT=kT[:, kvh, ke * 128:(ke + 1) * 128],
                                    rhs=qT[:, i, 4 * kvh:4 * kvh + 4, :].rearrange("p h q -> p (h q)"), start=True, stop=True)],
                                    reads=[RkT, RqT], writes=[Rb[bs]])
                                S.op("scalar", lambda e, bs=bs, kvh=kvh, kc=kc, Pb=Pb: e.activation(
                                    out=Pb[:, kvh, kc, :], in_=bank(bs)[:, 0:512], func=AF.Exp, scale=0.125),
                                    reads=[Rb[bs]], writes=[RP[kvh]])
                                if kc != 1:
                                    if kc == 0:
                                        mi = 2 if t == 0 else 0
                                    else:
                                        mi = 3 if t == NH - 1 else 1
                                    pv = Pb[:, kvh, kc, :].rearrange("p (h q) -> p h q", h=4)
                                    S.op("vector", lambda e, pv=pv, mi=mi: e.tensor_tensor(
                                        out=pv, in0=pv, in1=bcast_mid(masks[:, mi, :], 4), op=ALU.mult),
                                        reads=[RP[kvh], Rconst], writes=[RP[kvh]])
                            fns = []
                            for hh in range(4):
                                h8 = 4 * kvh + hh
                                o0 = (h8 // 4) * 512 + (h8 % 4) * 80
                                for kc in range(3):
                                    ke = e_ - 1 + kc
                                    fns.append(lambda e, pp=pp, o0=o0, kvh=kvh, kc=kc, hh=hh, ke=ke, Pb=Pb: e.matmul(
                                        PS[pp][:, o0:o0 + 65], lhsT=Pb[:, kvh, kc, hh * 128:(hh + 1) * 128],
                                        rhs=vaug[:, ke, kvh, 0:65], start=(kc == 0), stop=(kc == 2)))
                            S.group("tensor", fns, reads=[RP[kvh], Rv], writes=[Rb[2 * pp + kvh]])
                        for half in range(2):
                            pov = PS[pp][:, half * 512:half * 512 + 320].rearrange("p (h d) -> p h d", d=80)
                            S.op("vector", lambda e, pov=pov, half=half: e.tensor_tensor(
                                out=dsum[:, 4 * half:4 * half + 4], in0=pov[:, :, 64], in1=esink[:, l, 4 * half:4 * half + 4], op=ALU.add),
                                reads=[Rb[2 * pp + half], Rconst], writes=[Rds])
                        S.op("vector", lambda e: e.reciprocal(out=rinv[:, :], in_=dsum[:, :]), reads=[Rds], writes=[Rds])
                        for half in range(2):
                            pov = PS[pp][:, half * 512:half * 512 + 320].rearrange("p (h d) -> p h d", d=80)
                            S.op("vector", lambda e, pov=pov, half=half: e.tensor_tensor(
                                out=Osb[:, half * 256:(half + 1) * 256].rearrange("p (h d) -> p h d", h=4),
                                in0=pov[:, :, 0:64], in1=bcast_last(rinv[:, 4 * half:4 * half + 4], 64), op=ALU.mult),
                                reads=[Rb[2 * pp + half], Rds], writes=[RO])
                        pT3 = bank(3).bitcast(BF16)
                        S.group("tensor", [
                            (lambda e, c=c: e.transpose(out=pT3[:, c * 128:(c + 1) * 128], in_=Osb[:, c * 128:(c + 1) * 128], identity=ident[:]))
                            for c in range(4)], reads=[RO, Rconst], writes=[Rb[3]])
                        S.op("scalar", lambda e, i=i: e.copy(out=OT[:, :, i * 128:(i + 1) * 128],
                                                             in_=pT3[:, 0:512].rearrange("p (c t) -> p c t", c=4)),
                             reads=[Rb[3]], writes=[ROT])
                    if MS < 5:
                        for _ in range(5):
                            WS.get(WS.items[WS.next][0])
                            WS.release()
                        return
                    for br_ in range(2):
                        sw_, Rsw = WS.get("B")
                        sg_, Rsg = WS.get("A")
                        wbr = sw_[:, :].rearrange("p (c n) -> p c n", c=4)
                        wg = sg_[:, :].rearrange("p (c n) -> p c n", c=NCH)
                        srcT, Rsrc = (OT, ROT) if br_ == 0 else (mixed, Rmixed)
                        for m in range(NCH):
                            by = c2["y"] % 2
                            c2["y"] += 1
                            bg = 2 + c2["g"] % 2
                            k = c2["g"] % 2
                            c2["g"] += 1
                            S.group("tensor", [
                                (lambda e, c=c, by=by, m=m, wbr=wbr, srcT=srcT: e.matmul(
                                    bank(by)[:, 0:TO], lhsT=wbr[:, c, m * 128:(m + 1) * 128], rhs=srcT[:, c, 0:TO],
                                    start=(c == 0), stop=(c == 3))) for c in range(4)],
                                reads=[Rsw, Rsrc], writes=[Rb[by]])
                            S.group("tensor", [
                                (lambda e, c=c, bg=bg, m=m, wg=wg: e.matmul(
                                    bank(bg)[:, 0:TO], lhsT=wg[:, c, m * 128:(m + 1) * 128], rhs=uT[:, c, 128:128 + TO],
                                    start=(c == 0), stop=(c == NCH - 1))) for c in range(NCH)],
                                reads=[Rsg] + uT_regs(128, 128 + TO), writes=[Rb[bg]])
                            S.op("scalar", lambda e, bg=bg, k=k, m=m, br_=br_: e.activation(
                                out=sig[k][:, 0:TO], in_=bank(bg)[:, 0:TO], func=AF.Sigmoid,
                                bias=bg_all[:, l, 8 * br_ + m:8 * br_ + m + 1]), reads=[Rb[bg], Rconst], writes=[Rsig[k]])
                            if br_ == 0:
                                S.op("vector", lambda e, by=by, k=k, m=m: e.tensor_tensor(
                                    out=za[:, m, 0:TO], in0=sig[k][:, 0:TO], in1=bank(by)[:, 0:TO], op=ALU.mult),
                                    reads=[Rsig[k], Rb[by]], writes=[Rza])
                            else:
                                S.op("vector", lambda e, by=by, k=k: e.tensor_tensor(
                                    out=tg[:, 0:TO], in0=sig[k][:, 0:TO], in1=bank(by)[:, 0:TO], op=ALU.mult),
                                    reads=[Rsig[k], Rb[by]], writes=[Rtg])
                                S.op("vector", lambda e, m=m: e.tensor_tensor(
                                    out=zT[:, m, 0:TO], in0=tg[:, 0:TO], in1=za[:, m, 0:TO], op=ALU.add),
                                    reads=[Rtg, Rza], writes=[RzT])
                        WS.release()
                    so_, Rso = WS.get("A")
                    wo = so_[:, :].rearrange("p (c n) -> p c n", c=NCH)
                    for i in range(nt):
                        ti = m0 + i + 2
                        pd = 2 + c2["d"] % 2
                        c2["d"] += 1
                        S.group("tensor", [
                            (lambda e, pd=pd, half=half, m=m, i=i: e.matmul(
                                PS[pd][:, half * 512:(half + 1) * 512], lhsT=zT[:, m, i * 128:(i + 1) * 128],
                                rhs=wo[:, m, half * 512:(half + 1) * 512], start=(m == 0), stop=(m == NCH - 1)))
                            for half in range(2) for m in range(NCH)],
                            reads=[RzT, Rso], writes=[Rb[2 * pd], Rb[2 * pd + 1]])
                        for half in range(2):
                            S.op("vector", lambda e, pd=pd, half=half, i=i: e.tensor_tensor(
                                out=xblk[:, i, half * 512:(half + 1) * 512], in0=PS[pd][:, half * 512:(half + 1) * 512],
                                in1=xblk[:, i, half * 512:(half + 1) * 512], op=ALU.add),
                                reads=[Rb[2 * pd + half], Rx[i]], writes=[Rx[i]])
                        S.dma("sync", ssem[i], hdst[ti * 128:(ti + 1) * 128, :], xblk[:, i, :], reads=[Rx[i]], writes=[Rhdst[ti]])
                    WS.release()

                for (m0, m1) in split_blocks(lo, hi, MB):
                    do_block(m0, m1)
                S.barrier()

        out_toks = []
        cur = 0
        for pi, (kind, l, which, lo, hi) in enumerate(phases):
            if kind == "ffn":
                final = (pi == 3 * DEPTH - 1)
                toks = ffn_phase(l, which, lo, hi, first=(pi == 0), final=final, hi_=cur)
                out_toks += toks
            else:
                mixer_phase(l, lo, hi, cur)
                cur = 1 - cur
        hres, Rhres = Hb[cur], RH[cur]
        if stop_after is not None and stop_after < 3 * DEPTH:
            with ExitStack() as ps:
                tb = ps.enter_context(nc.sbuf_tensor("dbg", [128, D], F32))
                Rt = Region()
                dsem = newsem()
                for t in range(NH):
                    S.dma("sync", dsem, tb[:], hres[(t + 2) * 128:(t + 3) * 128, :], reads=[Rhres[t + 2]], writes=[Rt])
                    out_toks.append(S.dma("sync", dsem, out_d[t * 128:(t + 1) * 128, :], tb[:], reads=[Rt]))
                S.wait_all("sync", out_toks)
                S.run(block)
                return nc, S
        S.wait_all("sync", out_toks)
        S.run(block)
    return nc, S


def _const_tables(NH, S_len, half):
    NT = NH + 4
    ident = np.eye(128, dtype=np.float32).astype(ml_dtypes.bfloat16)
    rm = np.zeros((64, 64), np.float32)
    for m in range(16):
        k = m + 8 if m < 8 else m - 8
        rm[k, m] = 1.0
    rmat = rm.astype(ml_dtypes.bfloat16)
    j = np.arange(128)[:, None]
    i = np.arange(128)[None, :]
    maskP = (i <= j).astype(np.float32)
    maskN = (i >= j).astype(np.float32)
    zero = np.zeros_like(maskP)
    masks = np.stack([maskP, maskN, maskP if half == 1 else zero, maskN if half == 0 else zero]).astype(ml_dtypes.bfloat16)
    pos = (half * NH * 128 - 256 + np.arange(NT * 128)).astype(np.float32)
    inv_freq = (1.0 / (ROPE_THETA ** (np.arange(0, 16, 2, dtype=np.float32) / np.float32(16)))).astype(np.float32)
    ang = (pos[:, None] * inv_freq[None, :]).astype(np.float32)
    cos = np.cos(ang.astype(np.float64)).astype(np.float32).T
    sin = np.sin(ang.astype(np.float64)).astype(np.float32).T
    ropeC = np.ones((64, NT * 128), np.float32)
    ropeS = np.zeros((64, NT * 128), np.float32)
    ropeC[0:8] = cos
    ropeC[8:16] = cos
    ropeS[0:8] = -sin
    ropeS[8:16] = sin
    pfix = np.zeros((128, 4, 16), np.float32)
    for g in range(4):
        w = 2 << g
        hw = w // 2
        for c in range(8):
            t0 = c
            t1 = S_len - 8 + c
            c0 = min(t0 + hw, S_len) - max(t0 - hw, 0)
            c1 = min(t1 + hw, S_len) - max(t1 - hw, 0)
            pfix[:, g, c] = 1.0 / c0 if half == 0 else 1.0 / w
            pfix[:, g, 8 + c] = 1.0 / c1 if half == 1 else 1.0 / w
    valid = np.ones((128, 2), np.float32)
    valid[:, 0] = 0.0 if half == 0 else 1.0
    valid[:, 1] = 0.0 if half == 1 else 1.0
    return dict(ident=ident, rmat=rmat, masks=masks, ropeC=ropeC, ropeS=ropeS, pfix=pfix, valid=valid)


def make_in_maps(inputs):
    x = np.asarray(inputs["x"], dtype=np.float32)
    Bn, S_len, _ = x.shape
    NH = S_len // 256
    shared = {}
    for n in WSHAPES:
        if n in inputs:
            shared[n] = np.ascontiguousarray(np.asarray(inputs[n], dtype=np.float32))
    bg = np.asarray(inputs["b_gate"], np.float32)
    shared["b_gate_t"] = np.ascontiguousarray(bg.reshape(DEPTH, 16, 128).transpose(0, 2, 1))
    psc = np.asarray(inputs["pool_scale"], np.float32)
    shared["pool_scale_t"] = np.ascontiguousarray(psc.reshape(DEPTH, 4, 128).transpose(0, 2, 1))
    sk = np.asarray(inputs["attn_sink"], np.float32)
    shared["sink_b"] = np.ascontiguousarray(np.broadcast_to(sk[:, None, :], (DEPTH, 128, NHEADS)))
    consts = [_const_tables(NH, S_len, h) for h in range(2)]
    in_maps = []
    for b in range(Bn):
        for half in range(2):
            xp = np.zeros(((NH + 4) * 128, D), np.float32)
            g0 = half * NH * 128 - 256
            lo = max(g0, 0)
            hi = min(g0 + (NH + 4) * 128, S_len)
            xp[lo - g0:hi - g0] = x[b, lo:hi]
            m = dict(shared)
            m.update(consts[half])
            m["x_pad"] = xp
            in_maps.append(m)
    return in_maps, Bn, S_len, NH


_PROG_CACHE = {}


def kernel(**inputs):
    in_maps, Bn, S_len, NH = make_in_maps(inputs)
    if NH not in _PROG_CACHE:
        _PROG_CACHE[NH] = build_program(NH)[0]
    nc = _PROG_CACHE[NH]
    res = run_bass_kernel_spmd(nc, in_maps, core_ids=list(range(len(in_maps))))
    out = np.empty((Bn, S_len, D), np.float32)
    for b in range(Bn):
        for half in range(2):
            out[b, half * NH * 128:(half + 1) * NH * 128] = res.results[b * 2 + half]["out"]
    return out
```

```python
import os
import numpy as np
import ml_dtypes
import concourse.bass as bass
import concourse.mybir as mybir
from concourse.bass_utils import run_bass_kernel_spmd

F32 = mybir.dt.float32
BF16 = mybir.dt.bfloat16
ALU = mybir.AluOpType
AF = mybir.ActivationFunctionType
AX = mybir.AxisListType

ENGINES = ("sync", "scalar", "gpsimd", "vector", "tensor")


class Region:
    __slots__ = ("w", "r", "name", "excl")

    def __init__(self, name="", excl=False):
        self.w = None
        self.r = {}
        self.name = name
        self.excl = excl


class Sched:
    def __init__(self, nc, eng_sems):
        self.nc = nc
        self.q = {e: [] for e in ENGINES}
        self.sem = eng_sems
        self.cnt = {e: 0 for e in ENGINES}
        self.waited = {e: {} for e in ENGINES}
        self.dma_cnt = {}
        self._all_dma_sems = {}
        self.n_inst = 0

    def _collect(self, reads, writes, extra):
        deps = {}

        def add(t):
            if t is None:
                return
            k, s, v = t
            if k not in deps or deps[k][1] < v:
                deps[k] = (s, v)

        for R in reads:
            add(R.w)
            if R.excl:
                for k, (s, v) in R.r.items():
                    add((k, s, v))
        for R in writes:
            add(R.w)
            for k, (s, v) in R.r.items():
                add((k, s, v))
        for t in extra:
            add(t)
        return deps

    def _emit_waits(self, eng, deps):
        wd = self.waited[eng]
        for k, (s, v) in deps.items():
            if wd.get(k, 0) >= v:
                continue
            wd[k] = v
            self.q[eng].append(lambda e, s=s, v=v: e.wait_ge(s, v))
            self.n_inst += 1

    def _commit(self, tok, reads, writes):
        k, s, v = tok
        for R in reads:
            if R.excl:
                R.w = tok
                R.r = {}
            elif k not in R.r or R.r[k][1] < v:
                R.r[k] = (s, v)
        for R in writes:
            R.w = tok
            R.r = {}

    def op(self, eng, fn, reads=(), writes=(), extra=()):
        deps = self._collect(reads, writes, extra)
        self._emit_waits(eng, deps)
        self.cnt[eng] += 1
        s = self.sem[eng]
        tok = (eng, s, self.cnt[eng])
        self.q[eng].append(lambda e, fn=fn, s=s: fn(e).then_inc(s, 1))
        self.n_inst += 1
        self._commit(tok, reads, writes)
        return tok

    def group(self, eng, fns, reads=(), writes=(), extra=()):
        deps = self._collect(reads, writes, extra)
        self._emit_waits(eng, deps)
        self.cnt[eng] += 1
        s = self.sem[eng]
        tok = (eng, s, self.cnt[eng])
        for fn in fns[:-1]:
            self.q[eng].append(lambda e, fn=fn: fn(e))
        last = fns[-1]
        self.q[eng].append(lambda e, fn=last, s=s: fn(e).then_inc(s, 1))
        self.n_inst += len(fns)
        self._commit(tok, reads, writes)
        return tok

    def dma(self, eng, sem, out, in_, reads=(), writes=(), extra=(), n=1, fn=None):
        deps = self._collect(reads, writes, extra)
        self._emit_waits(eng, deps)
        key = ("dma", id(sem))
        self._all_dma_sems[key] = sem
        self.dma_cnt[key] = self.dma_cnt.get(key, 0) + 16
        tok = (key, sem, self.dma_cnt[key])
        if fn is None:
            self.q[eng].append(lambda e, o=out, i=in_, s=sem: e.dma_start(out=o, in_=i).then_inc(s, 16))
        else:
            self.q[eng].append(lambda e, fn=fn, s=sem: fn(e).then_inc(s, 16))
        self.n_inst += 1
        self._commit(tok, reads, writes)
        return tok

    def wait_all(self, eng, toks):
        deps = {}
        for t in toks:
            if t is None:
                continue
            k, s, v = t
            if k not in deps or deps[k][1] < v:
                deps[k] = (s, v)
        self._emit_waits(eng, deps)

    def run(self, block):
        q = self.q

        @block.sync
        def _(e):
            for f in q["sync"]:
                f(e)

        @block.scalar
        def _(e):
            for f in q["scalar"]:
                f(e)

        @block.gpsimd
        def _(e):
            for f in q["gpsimd"]:
                f(e)

        @block.vector
        def _(e):
            for f in q["vector"]:
                f(e)

        @block.tensor
        def _(e):
            for f in q["tensor"]:
                f(e)

    def dma_multi(self, eng, sem, pairs, reads=(), writes=(), extra=()):
        deps = self._collect(reads, writes, extra)
        self._emit_waits(eng, deps)
        key = ("dma", id(sem))
        self._all_dma_sems[key] = sem
        for (o, i) in pairs:
            self.dma_cnt[key] = self.dma_cnt.get(key, 0) + 16
            self.q[eng].append(lambda e, o=o, i=i, s=sem: e.dma_start(out=o, in_=i).then_inc(s, 16))
            self.n_inst += 1
        tok = (key, sem, self.dma_cnt[key])
        self._commit(tok, reads, writes)
        return tok

    def barrier(self):
        toks = []
        for e in ENGINES:
            if self.cnt[e] > 0:
                toks.append((e, self.sem[e], self.cnt[e]))
        for key, v in self.dma_cnt.items():
            toks.append((key, self._all_dma_sems[key], v))
        for e in ENGINES:
            self.wait_all(e, toks)


D = 1024
DFF = 2816
NCH = D // 128
NJ = DFF // 128
GROUPS = [(0, 4), (4, 4), (8, 4), (12, 4), (16, 3), (19, 3)]
DEPTH = 2
NHEADS = 8
FB = 12
MB = 4
EPS = 1e-6
ROPE_THETA = 500000.0

WSHAPES = {
    "ffn1_norm": [DEPTH, D], "ffn1_w1": [DEPTH, D, DFF], "ffn1_w3": [DEPTH, D, DFF], "ffn1_w2": [DEPTH, DFF, D],
    "mix_norm": [DEPTH, D], "w_in": [DEPTH, D, 3328], "pool_w": [DEPTH, 4, 128, 128],
    "w_branch_attn": [DEPTH, 512, D], "w_branch_pool": [DEPTH, 512, D], "w_out": [DEPTH, D, D],
    "ffn2_norm": [DEPTH, D], "ffn2_w1": [DEPTH, D, DFF], "ffn2_w3": [DEPTH, D, DFF], "ffn2_w2": [DEPTH, DFF, D],
    "final_norm": [D],
    "b_gate_t": [DEPTH, 128, 16], "pool_scale_t": [DEPTH, 128, 4], "sink_b": [DEPTH, 128, 8],
}


def split_blocks(lo, hi, maxb):
    n = hi - lo
    nb = -(-n // maxb)
    base, rem = n // nb, n % nb
    out, s = [], lo
    for i in range(nb):
        sz = base + (1 if i < rem else 0)
        out.append((s, s + sz))
        s += sz
    return out


def subs_of(T):
    return [(a, min(a + 512, T)) for a in range(0, T, 512)]


class WStream:
    def __init__(self, S, slots, regs, sems):
        self.S, self.slots, self.regs, self.sems = S, slots, regs, sems
        self.items = []
        self.kidx = []
        self.count = {"A": 0, "B": 0}
        self.emitted = 0
        self.next = 0
        self.released = 0

    def add(self, kind, pairs_fn):
        self.items.append((kind, pairs_fn))
        self.kidx.append(self.count[kind])
        self.count[kind] += 1

    def _prev_occupant(self, j):
        kind, _ = self.items[j]
        nslots = len(self.slots[kind])
        want = self.kidx[j] - nslots
        if want < 0:
            return -1
        for p in range(j - 1, -1, -1):
            if self.items[p][0] == kind and self.kidx[p] == want:
                return p
        return -1

    def _emit(self, j):
        kind, pairs_fn = self.items[j]
        s = self.kidx[j] % len(self.slots[kind])
        self.S.dma_multi("gpsimd", self.sems[kind][s], pairs_fn(self.slots[kind][s]), writes=[self.regs[kind][s]])

    def get(self, kind):
        i = self.next
        assert self.items[i][0] == kind, (i, self.items[i][0], kind)
        self.next += 1
        while self.emitted <= i:
            assert self._prev_occupant(self.emitted) < self.released, "slot still in use"
            self._emit(self.emitted)
            self.emitted += 1
        s = self.kidx[i] % len(self.slots[kind])
        return self.slots[kind][s], self.regs[kind][s]

    def release(self):
        self.released = self.next
        while self.emitted < len(self.items) and self._prev_occupant(self.emitted) < self.released:
            self._emit(self.emitted)
            self.emitted += 1


def build_program(NH, stop_after=None):
    from contextlib import ExitStack
    nc = bass.Bass("TRN2", target_bir_lowering=False)
    NT = NH + 4

    def din(name, shape, dt=F32):
        return nc.dram_tensor(name, list(shape), dt, kind="ExternalInput").ap()

    x_pad = din("x_pad", [NT * 128, D])
    W = {n: din(n, shp) for n, shp in WSHAPES.items()}
    ident_d = din("ident", [128, 128], BF16)
    rmat_d = din("rmat", [64, 64], BF16)
    masks_d = din("masks", [4, 128, 128], BF16)
    ropeC_d = din("ropeC", [64, NT * 128])
    ropeS_d = din("ropeS", [64, NT * 128])
    pfix_d = din("pfix", [128, 4, 16])
    valid_d = din("valid", [128, 2])
    out_d = nc.dram_tensor("out", [NH * 128, D], F32, kind="ExternalOutput").ap()
    Hb = [nc.dram_tensor("hres%d" % i, [NT * 128, D], F32, kind="Internal").ap() for i in range(2)]

    with ExitStack() as es:
        E = es.enter_context

        def sb(name, shape, dt):
            return E(nc.sbuf_tensor(name, list(shape), dt))

        sems = {e: E(nc.semaphore("s_" + e)) for e in ENGINES}
        S = Sched(nc, sems)
        _semn = [0]

        def newsem():
            _semn[0] += 1
            return E(nc.semaphore("d%d" % _semn[0]))

        slotA = [sb("slotA%d" % i, [128, 8192], BF16) for i in range(2)]
        slotB = [sb("slotB%d" % i, [128, 4096], BF16) for i in range(2)]
        RA = [Region("A0"), Region("A1")]
        RB = [Region("B0"), Region("B1")]
        WS = WStream(S, {"A": slotA, "B": slotB}, {"A": RA, "B": RB},
                     {"A": [newsem(), newsem()], "B": [newsem(), newsem()]})
        uT = sb("uT", [128, NCH, FB * 128], BF16)
        RuT = [Region("uT%d" % i) for i in range(FB)]
        gt = sb("gt", [128, D], F32); Rgt = Region("gt"); sem_g = newsem()
        Rgt2 = Region("gt2")
        utmp = [sb("utmp%d" % i, [128, D], BF16) for i in range(4)]; Rutmp = [Region() for _ in range(4)]
        st = sb("st", [128, 4, 4], F32); Rst = [Region() for _ in range(4)]
        ident = sb("ident_s", [128, 128], BF16); Rconst = Region("const")
        rmat = sb("rmat_s", [64, 64], BF16)
        masks = sb("masks_s", [128, 4, 128], BF16)
        pfix = sb("pfix_s", [128, 4, 16], F32)
        valid = sb("valid_s", [128, 2], F32)
        bg_all = sb("bg_s", [128, DEPTH, 16], F32)
        psc_all = sb("psc_s", [128, DEPTH, 4], F32)
        esink = sb("esink_s", [128, DEPTH, 8], F32)
        PS = [E(nc.psum_tensor("ps%d" % i, [128, 1024], F32)) for i in range(4)]
        Rb = [Region("bank%d" % i, excl=True) for i in range(8)]

        def bank(k):
            return PS[k // 2][:, (k % 2) * 512:(k % 2) * 512 + 512]

        RH = [[Region("hres%d_%d" % (b, i)) for i in range(NT)] for b in range(2)]
        xsem_g = [newsem() for _ in range(FB)]
        ssem_g = [newsem() for _ in range(FB)]
        sem_g2 = newsem()
        rsem_g = newsem()
        block = E(nc.Block())

        sem_c = newsem()
        S.dma_multi("sync", sem_c, [
            (ident[:], ident_d), (rmat[:], rmat_d), (masks[:], masks_d.rearrange("m p q -> p m q")),
            (pfix[:], pfix_d), (valid[:], valid_d),
            (bg_all[:], W["b_gate_t"].rearrange("l p m -> p l m")),
            (psc_all[:], W["pool_scale_t"].rearrange("l p m -> p l m")),
            (esink[:], W["sink_b"].rearrange("l p m -> p l m")),
        ], writes=[Rconst])
        S.op("scalar", lambda e: e.activation(out=esink[:], in_=esink[:], func=AF.Exp), reads=[Rconst], writes=[Rconst])

        def gload(dst, Rd, name, off, sem=None):
            src = bass.AP(W[name].tensor, off, [[0, 128], [1, D]])
            S.dma("sync", sem_g if sem is None else sem, dst[:], src, writes=[Rd])

        def ffn_items(l, which):
            pre = "ffn%d" % which
            w1, w3, w2 = W[pre + "_w1"][l], W[pre + "_w3"][l], W[pre + "_w2"][l]
            for (j0, ng) in GROUPS:
                def pa(slot, j0=j0, ng=ng, w1=w1, w3=w3):
                    v = slot[:, :].rearrange("p (a c n) -> p a c n", a=2, c=NCH)
                    return [(v[:, 0, :, 0:ng * 128], w1[:, j0 * 128:(j0 + ng) * 128].rearrange("(c p) n -> p c n", p=128)),
                            (v[:, 1, :, 0:ng * 128], w3[:, j0 * 128:(j0 + ng) * 128].rearrange("(c p) n -> p c n", p=128))]

                def pb(slot, j0=j0, ng=ng, w2=w2):
                    v = slot[:, :].rearrange("p (j n) -> p j n", j=4)
                    return [(v[:, 0:ng, :], w2[j0 * 128:(j0 + ng) * 128, :].rearrange("(j p) n -> p j n", p=128))]
                WS.add("A", pa)
                WS.add("B", pb)

        def mixer_items(l):
            win = W["w_in"][l]

            def a_qkv(slot):
                v = slot[:, 0:6144].rearrange("p (c n) -> p c n", c=NCH)
                pw = slot[:, 6144:6656].rearrange("p (g d) -> p g d", g=4)
                return [(v, win[:, 0:768].rearrange("(c p) n -> p c n", p=128)),
                        (pw, W["pool_w"][l].rearrange("g c d -> c g d"))]

            def b_p(slot):
                return [(slot[:, :].rearrange("p (c n) -> p c n", c=NCH), win[:, 768:1280].rearrange("(c p) n -> p c n", p=128))]

            def b_bra(slot):
                return [(slot[:, :].rearrange("p (c n) -> p c n", c=4), W["w_branch_attn"][l].rearrange("(c p) n -> p c n", p=128))]

            def a_ga(slot):
                return [(slot[:, :].rearrange("p (c n) -> p c n", c=NCH), win[:, 1280:2304].rearrange("(c p) n -> p c n", p=128))]

            def b_brp(slot):
                return [(slot[:, :].rearrange("p (c n) -> p c n", c=4), W["w_branch_pool"][l].rearrange("(c p) n -> p c n", p=128))]

            def a_gb(slot):
                return [(slot[:, :].rearrange("p (c n) -> p c n", c=NCH), win[:, 2304:3328].rearrange("(c p) n -> p c n", p=128))]

            def a_out(slot):
                return [(slot[:, :].rearrange("p (c n) -> p c n", c=NCH), W["w_out"][l].rearrange("(c p) n -> p c n", p=128))]
            for kind, fn in (("A", a_qkv), ("B", b_p), ("B", b_bra), ("A", a_ga), ("B", b_brp), ("A", a_gb), ("A", a_out)):
                WS.add(kind, fn)

        phases = []
        for l in range(DEPTH):
            h_in = DEPTH - l
            phases.append(("ffn", l, 1, -h_in, NH + h_in))
            phases.append(("mix", l, 0, -(h_in - 1), NH + h_in - 1))
            phases.append(("ffn", l, 2, -(h_in - 1), NH + h_in - 1))
        if stop_after is not None:
            phases = phases[:stop_after]
        for (kind, l, which, lo, hi) in phases:
            if kind == "ffn":
                for _ in split_blocks(lo, hi, FB):
                    ffn_items(l, which)
            else:
                for _ in split_blocks(lo, hi, MB):
                    mixer_items(l)

        cnt = {"n": 0, "t": 0}

        def norm_phase(tiles, g, Rg):
            n = len(tiles)
            info = {}

            def p1(t):
                xap, Rx, col0 = tiles[t]
                k = cnt["n"] % 4
                cnt["n"] += 1
                info[t] = k
                S.op("scalar", lambda e: e.activation(out=utmp[k][:], in_=xap, func=AF.Square, accum_out=st[:, k, 0:1]),
                     reads=[Rx], writes=[Rutmp[k], Rst[k]])
                S.op("vector", lambda e: e.tensor_scalar(out=st[:, k, 1:2], in0=st[:, k, 0:1], scalar1=1.0 / D, scalar2=EPS,
                                                          op0=ALU.mult, op1=ALU.add), reads=[Rst[k]], writes=[Rst[k]])

            def p2(t):
                xap, Rx, col0 = tiles[t]
                k = info[t]
                S.op("scalar", lambda e: e.activation(out=st[:, k, 2:3], in_=st[:, k, 1:2], func=AF.Sqrt),
                     reads=[Rst[k]], writes=[Rst[k]])
                S.op("vector", lambda e: e.reciprocal(out=st[:, k, 3:4], in_=st[:, k, 2:3]), reads=[Rst[k]], writes=[Rst[k]])
                S.op("vector", lambda e: e.scalar_tensor_tensor(out=utmp[k][:], in0=xap, scalar=st[:, k, 3:4], in1=g[:],
                                                                 op0=ALU.mult, op1=ALU.mult),
                     reads=[Rx, Rst[k], Rg], writes=[Rutmp[k]])
                bk = (4, 6, 5, 7)[cnt["t"] % 4]
                cnt["t"] += 1
                info[t] = bk
                pT = bank(bk).bitcast(BF16)
                S.group("tensor", [
                    (lambda e, c=c: e.transpose(out=pT[:, c * 128:(c + 1) * 128], in_=utmp[k][:, c * 128:(c + 1) * 128],
                                                identity=ident[:])) for c in range(NCH)],
                    reads=[Rutmp[k], Rconst], writes=[Rb[bk]])

            def p3(t):
                xap, Rx, col0 = tiles[t]
                bk = info[t]
                pT = bank(bk).bitcast(BF16)
                S.op("scalar", lambda e: e.copy(out=uT[:, :, col0:col0 + 128], in_=pT.rearrange("p (c t) -> p c t", c=NCH)),
                     reads=[Rb[bk]], writes=[RuT[col0 // 128]])

            for s_ in range(n + 2):
                if s_ < n:
                    p1(s_)
                if 0 <= s_ - 1 < n:
                    p2(s_ - 1)
                if 0 <= s_ - 2 < n:
                    p3(s_ - 2)

        def uT_regs(n0, n1):
            return [RuT[i] for i in range(n0 // 128, (n1 + 127) // 128)]

        def ffn_phase(l, which, lo, hi, first, final, hi_):
            pre = "ffn%d" % which
            hres = Hb[hi_]
            Rhres = RH[hi_]
            src = x_pad if first else hres
            with ExitStack() as ps:
                uid = "_f%d%d" % (l, which)
                xblk = ps.enter_context(nc.sbuf_tensor("xblk" + uid, [128, FB, D], F32))
                hbuf = [ps.enter_context(nc.sbuf_tensor("hb%d%s" % (i, uid), [128, 4, FB * 128], BF16)) for i in range(2)]
                silu = [ps.enter_context(nc.sbuf_tensor("silu%d%s" % (i, uid), [128, 512], F32)) for i in range(2)]
                Rx = [Region() for _ in range(FB)]
                Rh = [Region(), Region()]
                Rsilu = [Region(), Region()]
                xsem = xsem_g
                ssem = ssem_g
                gload(gt, Rgt, pre + "_norm", l * D)
                if final:
                    gt2 = ps.enter_context(nc.sbuf_tensor("gt2", [128, D], F32))
                    gload(gt2, Rgt2, "final_norm", 0, sem=sem_g2)
                cf = {"gc": 0, "uc": 0, "dc": 0}
                out_toks = []

                def do_block(b0, b1):
                    gc, uc, dc = cf["gc"], cf["uc"], cf["dc"]
                    nt = b1 - b0
                    T = nt * 128
                    for i in range(nt):
                        t = b0 + i
                        ti = t + 2
                        S.dma("sync", xsem[i], xblk[:, i, :], src[ti * 128:(ti + 1) * 128, :],
                              reads=([] if first else [Rhres[ti]]), writes=[Rx[i]])
                        if l >= 1 and which == 1 and (t < 0 or t >= NH):
                            col = 0 if t < 0 else 1
                            S.op("vector", lambda e, i=i, col=col: e.tensor_scalar(
                                out=xblk[:, i, :], in0=xblk[:, i, :], scalar1=valid[:, col:col + 1], scalar2=None,
                                op0=ALU.mult), reads=[Rx[i], Rconst], writes=[Rx[i]])
                    norm_phase([(xblk[:, i, :], Rx[i], i * 128) for i in range(nt)], gt, Rgt)
                    subs = subs_of(T)
                    for (j0, ng) in GROUPS:
                        sa, Rsa = WS.get("A")
                        sbw, Rsb = WS.get("B")
                        wa = sa[:, :].rearrange("p (a c n) -> p a c n", a=2, c=NCH)
                        wb = sbw[:, :].rearrange("p (j n) -> p j n", j=4)
                        hb, Rhb = hbuf[gc % 2], Rh[gc % 2]
                        gc += 1
                        for jj in range(ng):
                            for (n0, n1) in subs:
                                n = n1 - n0
                                ba, bb = (0, 1) if uc % 2 == 0 else (2, 3)
                                sl, Rsl = silu[uc % 2], Rsilu[uc % 2]
                                uc += 1
                                for (bk, a) in ((ba, 0), (bb, 1)):
                                    S.group("tensor", [
                                        (lambda e, c=c, bk=bk, a=a, jj=jj, n0=n0, n1=n1, n=n, wa=wa: e.matmul(
                                            bank(bk)[:, 0:n], lhsT=wa[:, a, c, jj * 128:(jj + 1) * 128], rhs=uT[:, c, n0:n1],
                                            start=(c == 0), stop=(c == NCH - 1))) for c in range(NCH)],
                                        reads=[Rsa] + uT_regs(n0, n1), writes=[Rb[bk]])
                                S.op("scalar", lambda e, ba=ba, sl=sl, n=n: e.activation(out=sl[:, 0:n], in_=bank(ba)[:, 0:n], func=AF.Silu),
                                     reads=[Rb[ba]], writes=[Rsl])
                                S.op("vector", lambda e, bb=bb, sl=sl, n=n, hb=hb, jj=jj, n0=n0, n1=n1: e.tensor_tensor(
                                    out=hb[:, jj, n0:n1], in0=sl[:, 0:n], in1=bank(bb)[:, 0:n], op=ALU.mult),
                                    reads=[Rsl, Rb[bb]], writes=[Rhb])
                        for i in range(nt):
                            pd = 2 + dc % 2
                            dc += 1
                            S.group("tensor", [
                                (lambda e, pd=pd, half=half, jj=jj, i=i, hb=hb, wb=wb, ng=ng: e.matmul(
                                    PS[pd][:, half * 512:(half + 1) * 512], lhsT=hb[:, jj, i * 128:(i + 1) * 128],
                                    rhs=wb[:, jj, half * 512:(half + 1) * 512], start=(jj == 0), stop=(jj == ng - 1)))
                                for half in range(2) for jj in range(ng)],
                                reads=[Rhb, Rsb], writes=[Rb[2 * pd], Rb[2 * pd + 1]])
                            for half in range(2):
                                S.op("vector", lambda e, pd=pd, half=half, i=i: e.scalar_tensor_tensor(
                                    out=xblk[:, i, half * 512:(half + 1) * 512], in0=PS[pd][:, half * 512:(half + 1) * 512],
                                    scalar=0.5, in1=xblk[:, i, half * 512:(half + 1) * 512], op0=ALU.mult, op1=ALU.add),
                                    reads=[Rb[2 * pd + half], Rx[i]], writes=[Rx[i]])
                        WS.release()
                    for i in range(nt):
                        t = b0 + i
                        ti = t + 2
                        if not final:
                            S.dma("sync", ssem[i], hres[ti * 128:(ti + 1) * 128, :], xblk[:, i, :],
                                  reads=[Rx[i]], writes=[Rhres[ti]])
                        else:
                            k = cnt["n"] % 4
                            cnt["n"] += 1
                            xap = xblk[:, i, :]
                            S.op("scalar", lambda e, xap=xap, k=k: e.activation(out=utmp[k][:], in_=xap, func=AF.Square, accum_out=st[:, k, 0:1]),
                                 reads=[Rx[i]], writes=[Rutmp[k], Rst[k]])
                            S.op("vector", lambda e, k=k: e.tensor_scalar(out=st[:, k, 1:2], in0=st[:, k, 0:1], scalar1=1.0 / D, scalar2=EPS,
                                                                          op0=ALU.mult, op1=ALU.add), reads=[Rst[k]], writes=[Rst[k]])
                            S.op("scalar", lambda e, k=k: e.activation(out=st[:, k, 2:3], in_=st[:, k, 1:2], func=AF.Sqrt),
                                 reads=[Rst[k]], writes=[Rst[k]])
                            S.op("vector", lambda e, k=k: e.reciprocal(out=st[:, k, 3:4], in_=st[:, k, 2:3]), reads=[Rst[k]], writes=[Rst[k]])
                            S.op("vector", lambda e, xap=xap, k=k: e.scalar_tensor_tensor(out=xap, in0=xap, scalar=st[:, k, 3:4], in1=gt2[:],
                                                                                         op0=ALU.mult, op1=ALU.mult),
                                 reads=[Rx[i], Rst[k], Rgt2], writes=[Rx[i]])
                            out_toks.append(S.dma("sync", ssem[i], out_d[t * 128:(t + 1) * 128, :], xap, reads=[Rx[i]]))
                    cf["gc"], cf["uc"], cf["dc"] = gc, uc, dc

                for (b0, b1) in split_blocks(lo, hi, FB):
                    do_block(b0, b1)
                S.barrier()
                return out_toks

        def bcast_last(ap, n):
            return bass.AP(ap.tensor, ap.offset, [list(d) for d in ap.ap] + [[0, n]])

        def bcast_mid(ap, n):
            d = [list(x) for x in ap.ap]
            return bass.AP(ap.tensor, ap.offset, [d[0], [0, n]] + d[1:])

        def mixer_phase(l, lo, hi, hsrc):
            hres, Rhres = Hb[hsrc], RH[hsrc]
            hdst, Rhdst = Hb[1 - hsrc], RH[1 - hsrc]
            with ExitStack() as ps:
                def pb_(name, shape, dt):
                    return ps.enter_context(nc.sbuf_tensor(name + "_m%d" % l, list(shape), dt))
                ME = MB + 2
                xblk = pb_("xblk_m", [128, MB, D], F32)
                xh = pb_("xh", [128, 2, D], F32)
                qT = pb_("qT", [64, MB, NHEADS, 128], BF16)
                kT = pb_("kT", [64, 2, ME * 128], BF16)
                vaug = pb_("vaug", [128, ME, 2, 80], BF16)
                rC = pb_("rC", [64, ME * 128], F32)
                rS = pb_("rS", [64, ME * 128], F32)
                Pb2 = [pb_("Pb%d" % i, [128, 2, 3, 512], BF16) for i in range(2)]
                Osb2 = [pb_("Osb%d" % i, [128, 512], BF16) for i in range(2)]
                OT = pb_("OT", [128, 4, MB * 128], BF16)
                za = pb_("za", [128, NCH, MB * 128], F32)
                zT = pb_("zT", [128, NCH, MB * 128], BF16)
                PW = ME * 128 + 32
                pbuf = pb_("pbuf", [128, PW], F32)
                sa = pb_("sa", [128, PW], F32)
                sbb = pb_("sbb", [128, PW], F32)
                pooled = pb_("pooled", [128, 4, MB * 128], BF16)
                mixed = pb_("mixed", [128, 4, MB * 128], BF16)
                tq2 = [pb_("tq%d" % i, [64, 512], BF16) for i in range(2)]
                t12 = [pb_("t1%d" % i, [64, 512], F32) for i in range(2)]
                t22 = [pb_("t2%d" % i, [64, 512], F32) for i in range(2)]
                sig = [pb_("sig%d" % i, [128, 512], F32) for i in range(2)]
                tg = pb_("tg", [128, 512], F32)
                dsum = pb_("dsum", [128, 8], F32)
                rinv = pb_("rinv", [128, 8], F32)
                tmp8 = pb_("tmp8", [128, 8], F32)
                Rx = [Region() for _ in range(MB)]
                Rxh = [Region(), Region()]
                RqT, RkT, Rv, Rrope = Region(), Region(), Region(), Region()
                RP2 = [[Region(), Region()], [Region(), Region()]]
                ROT, Rza, RzT = Region(), Region(), Region()
                RO2 = [Region(), Region()]
                Rp, Rsa, Rsb, Rpooled, Rmixed = Region(), Region(), Region(), Region(), Region()
                Rtg, Rds, Rt8 = Region(), Region(), Region()
                Rtq2, Rt12, Rt22 = [Region(), Region()], [Region(), Region()], [Region(), Region()]
                Rsig = [Region(), Region()]
                xsem = xsem_g
                ssem = ssem_g
                rsem = rsem_g
                gload(gt, Rgt, "mix_norm", l * D)
                S.op("vector", lambda e: e.memset(vaug[:].rearrange("p a b c -> p (a b c)"), 1.0), writes=[Rv])
                S.op("vector", lambda e: e.memset(pbuf[:], 0.0), writes=[Rp])
                c2 = {"q": 0, "r": 0, "v": 0, "s": 0, "po": 0, "y": 0, "g": 0, "d": 0}

                def rope_head(bq, n):
                    kk = c2["r"] % 2
                    br = 2 + c2["r"] % 2
                    c2["r"] += 1
                    tq = tq2[kk]
                    S.op("scalar", lambda e: e.copy(out=tq[:, 0:n], in_=bank(bq)[0:64, 0:n]), reads=[Rb[bq]], writes=[Rtq2[kk]])
                    return kk, br

                def rope_tail(bq, n, c0, dst, Rdst, kk, br):
                    tq, t1, t2 = tq2[kk], t12[kk], t22[kk]
                    Rtq, Rt1, Rt2 = Rtq2[kk], Rt12[kk], Rt22[kk]
                    S.group("tensor", [lambda e: e.matmul(bank(br)[0:64, 0:n], lhsT=rmat[:, :], rhs=tq[:, 0:n], start=True, stop=True)],
                            reads=[Rtq, Rconst], writes=[Rb[br]])
                    S.op("vector", lambda e: e.tensor_tensor(out=t1[:, 0:n], in0=bank(bq)[0:64, 0:n], in1=rC[:, c0:c0 + n], op=ALU.mult),
                         reads=[Rb[bq], Rrope], writes=[Rt1])
                    S.op("vector", lambda e: e.tensor_tensor(out=t2[:, 0:n], in0=bank(br)[0:64, 0:n], in1=rS[:, c0:c0 + n], op=ALU.mult),
                         reads=[Rb[br], Rrope], writes=[Rt2])
                    if len(dst.shape) == 3:
                        i0 = t1[:, 0:n].rearrange("p (a b) -> p a b", b=128)
                        i1 = t2[:, 0:n].rearrange("p (a b) -> p a b", b=128)
                    else:
                        i0, i1 = t1[:, 0:n], t2[:, 0:n]
                    S.op("vector", lambda e: e.tensor_tensor(out=dst, in0=i0, in1=i1, op=ALU.add),
                         reads=[Rt1, Rt2], writes=[Rdst])

                def do_block(m0, m1):
                    nt = m1 - m0
                    TO = nt * 128
                    TE = (nt + 2) * 128
                    for e_ in range(nt + 2):
                        ti = m0 - 1 + e_ + 2
                        if e_ == 0:
                            dst, Rd = xh[:, 0, :], Rxh[0]
                        elif e_ == nt + 1:
                            dst, Rd = xh[:, 1, :], Rxh[1]
                        else:
                            dst, Rd = xblk[:, e_ - 1, :], Rx[e_ - 1]
                        S.dma("sync", xsem[e_], dst, hres[ti * 128:(ti + 1) * 128, :], reads=[Rhres[ti]], writes=[Rd])
                    pc0 = (m0 + 1) * 128
                    S.dma_multi("sync", rsem, [(rC[:, 0:TE], ropeC_d[:, pc0:pc0 + TE]), (rS[:, 0:TE], ropeS_d[:, pc0:pc0 + TE])],
                                writes=[Rrope])
                    ntl = []
                    for e_ in range(nt + 2):
                        if e_ == 0:
                            xap, Rd = xh[:, 0, :], Rxh[0]
                        elif e_ == nt + 1:
                            xap, Rd = xh[:, 1, :], Rxh[1]
                        else:
                            xap, Rd = xblk[:, e_ - 1, :], Rx[e_ - 1]
                        ntl.append((xap, Rd, e_ * 128))
                    norm_phase(ntl, gt, Rgt)
                    MS = 9
                    sq_, Rsq = WS.get("A")
                    if MS < 2:
                        WS.release()
                        for _ in range(6):
                            WS.get(WS.items[WS.next][0])
                            WS.release()
                        return
                    wq = sq_[:, 0:6144].rearrange("p (c n) -> p c n", c=NCH)
                    poolw = sq_[:, 6144:6656].rearrange("p (g d) -> p g d", g=4)
                    units = []
                    for kvh in range(2):
                        for (n0, n1) in subs_of(TE):
                            units.append((512 + 64 * kvh, n0, n1, 0, kT[:, kvh, n0:n1], RkT))
                    for h in range(NHEADS):
                        for (n0, n1) in subs_of(TO):
                            units.append((64 * h, n0, n1, 128, qT[:, n0 // 128:n1 // 128, h, :], RqT))
                    pend = None
                    for (wc, n0, n1, off, dst, Rdst) in units:
                        n = n1 - n0
                        bq = c2["q"] % 2
                        c2["q"] += 1
                        S.group("tensor", [
                            (lambda e, c=c, bq=bq, n=n, n0=n0, n1=n1, wc=wc, off=off: e.matmul(
                                bank(bq)[0:64, 0:n], lhsT=wq[:, c, wc:wc + 64], rhs=uT[:, c, off + n0:off + n1],
                                start=(c == 0), stop=(c == NCH - 1))) for c in range(NCH)],
                            reads=[Rsq] + uT_regs(off + n0, off + n1), writes=[Rb[bq]])
                        kk, br = rope_head(bq, n)
                        if pend is not None:
                            rope_tail(*pend)
                        pend = (bq, n, off + n0, dst, Rdst, kk, br)
                    rope_tail(*pend)
                    for e_ in range(nt + 2):
                        bv = 5 + 2 * (c2["v"] % 2)
                        c2["v"] += 1
                        S.group("tensor", [
                            (lambda e, c=c, bv=bv, e_=e_: e.matmul(bank(bv)[:, 0:128], lhsT=uT[:, c, e_ * 128:(e_ + 1) * 128],
                                                                   rhs=wq[:, c, 640:768], start=(c == 0), stop=(c == NCH - 1)))
                            for c in range(NCH)], reads=[Rsq, RuT[e_]], writes=[Rb[bv]])
                        S.op("scalar", lambda e, bv=bv, e_=e_: e.copy(out=vaug[:, e_, :, 0:64],
                                                                      in_=bank(bv)[:, 0:128].rearrange("p (h d) -> p h d", h=2)),
                             reads=[Rb[bv]], writes=[Rv])
                    sp_, Rsp = WS.get("B")
                    if MS < 3:
                        WS.release()
                        for _ in range(5):
                            WS.get(WS.items[WS.next][0])
                            WS.release()
                        return
                    wp = sp_[:, :].rearrange("p (c n) -> p c n", c=NCH)
                    for g in range(4):
                        for (n0, n1) in subs_of(TE):
                            n = n1 - n0
                            bq = c2["q"] % 2
                            c2["q"] += 1
                            S.group("tensor", [
                                (lambda e, c=c, bq=bq, n=n, n0=n0, n1=n1, g=g: e.matmul(
                                    bank(bq)[:, 0:n], lhsT=wp[:, c, g * 128:(g + 1) * 128], rhs=uT[:, c, n0:n1],
                                    start=(c == 0), stop=(c == NCH - 1))) for c in range(NCH)],
                                reads=[Rsp] + uT_regs(n0, n1), writes=[Rb[bq]])
                            S.op("scalar", lambda e, bq=bq, n=n, n0=n0, n1=n1: e.copy(out=pbuf[:, 16 + n0:16 + n1], in_=bank(bq)[:, 0:n]),
                                 reads=[Rb[bq]], writes=[Rp])
                        K = g + 1
                        w = 1 << K
                        hw = w // 2
                        A = 16 + 128 - hw
                        B = A + TO
                        src, Rsrc = pbuf, Rp
                        for k in range(1, K + 1):
                            hi_ = B + sum(1 << i for i in range(k, K))
                            dstb, Rdst = (sa, Rsa) if k % 2 == 1 else (sbb, Rsb)
                            sh = 1 << (k - 1)
                            S.op("vector", lambda e, src=src, dstb=dstb, sh=sh, hi_=hi_, A=A: e.tensor_tensor(
                                out=dstb[:, A:hi_], in0=src[:, A:hi_], in1=src[:, A + sh:hi_ + sh], op=ALU.add),
                                reads=[Rsrc], writes=[Rdst])
                            src, Rsrc = dstb, Rdst
                        S.op("vector", lambda e, src=src, g=g, w=w, A=A, B=B: e.scalar_tensor_tensor(
                            out=pooled[:, g, 0:TO], in0=src[:, A:B], scalar=1.0 / w, in1=pbuf[:, 144:144 + TO],
                            op0=ALU.mult, op1=ALU.subtract), reads=[Rsrc, Rp], writes=[Rpooled])
                        edges = []
                        if m0 <= 0 < m1:
                            edges.append(((0 - m0) * 128, 0))
                        if m0 <= NH - 1 < m1:
                            edges.append(((NH - 1 - m0) * 128 + 120, 8))
                        for (c0, f0) in edges:
                            S.op("vector", lambda e, src=src, g=g, A=A, c0=c0, f0=f0: e.tensor_tensor(
                                out=tmp8[:, :], in0=src[:, A + c0:A + c0 + 8], in1=pfix[:, g, f0:f0 + 8], op=ALU.mult),
                                reads=[Rsrc, Rconst], writes=[Rt8])
                            S.op("vector", lambda e, g=g, c0=c0: e.tensor_tensor(
                                out=pooled[:, g, c0:c0 + 8], in0=tmp8[:, :], in1=pbuf[:, 144 + c0:144 + c0 + 8], op=ALU.subtract),
                                reads=[Rt8, Rp], writes=[Rpooled])
                    for g in range(4):
                        bm = 2 + c2["r"] % 2
                        c2["r"] += 1
                        S.group("tensor", [lambda e, bm=bm, g=g: e.matmul(bank(bm)[:, 0:TO], lhsT=poolw[:, g, :], rhs=pooled[:, g, 0:TO],
                                                                         start=True, stop=True)],
                                reads=[Rsq, Rpooled], writes=[Rb[bm]])
                        S.op("vector", lambda e, bm=bm, g=g: e.tensor_scalar(out=mixed[:, g, 0:TO], in0=bank(bm)[:, 0:TO],
                                                                            scalar1=psc_all[:, l, g:g + 1], scalar2=None, op0=ALU.mult),
                             reads=[Rb[bm], Rconst], writes=[Rmixed])
                    WS.release()
                    if MS < 4:
                        for _ in range(5):
                            WS.get(WS.items[WS.next][0])
                            WS.release()
                        return
                    def sx(u):
                        i, kvh = u // 2, u % 2
                        e_ = i + 1
                        t = m0 + i
                        Pb, RP = Pb2[i % 2], RP2[i % 2]
                        for kc in range(3):
                            ke = e_ - 1 + kc
                            bs = c2["s"] % 3
                            c2["s"] += 1
                            S.group("tensor", [lambda e, bs=bs, ke=ke: e.matmul(
                                bank(bs)[:, 0:512], lhsT=kT[:, kvh, ke * 128:(ke + 1) * 128],
                                rhs=qT[:, i, 4 * kvh:4 * kvh + 4, :].rearrange("p h q -> p (h q)"), start=True, stop=True)],
                                reads=[RkT, RqT], writes=[Rb[bs]])
                            S.op("scalar", lambda e, bs=bs, kc=kc: e.activation(
                                out=Pb[:, kvh, kc, :], in_=bank(bs)[:, 0:512], func=AF.Exp, scale=0.125),
                                reads=[Rb[bs]], writes=[RP[kvh]])
                            if kc != 1:
                                if kc == 0:
                                    mi = 2 if t == 0 else 0
                                else:
                                    mi = 3 if t == NH - 1 else 1
                                pv = Pb[:, kvh, kc, :].rearrange("p (h q) -> p h q", h=4)
                                S.op("vector", lambda e, pv=pv, mi=mi: e.tensor_tensor(
                                    out=pv, in0=pv, in1=bcast_mid(masks[:, mi, :], 4), op=ALU.mult),
                                    reads=[RP[kvh], Rconst], writes=[RP[kvh]])

                    def pvu(u):
                        i, kvh = u // 2, u % 2
                        e_ = i + 1
                        pp = 2 + i % 2
                        Pb, RP = Pb2[i % 2], RP2[i % 2]
                        fns = []
                        for hh in range(4):
                            h8 = 4 * kvh + hh
                            o0 = (h8 // 4) * 512 + (h8 % 4) * 80
                            for kc in range(3):
                                ke = e_ - 1 + kc
                                fns.append(lambda e, o0=o0, kc=kc, hh=hh, ke=ke: e.matmul(
                                    PS[pp][:, o0:o0 + 65], lhsT=Pb[:, kvh, kc, hh * 128:(hh + 1) * 128],
                                    rhs=vaug[:, ke, kvh, 0:65], start=(kc == 0), stop=(kc == 2)))
                        S.group("tensor", fns, reads=[RP[kvh], Rv], writes=[Rb[2 * pp + kvh]])

                    def fd(i):
                        pp = 2 + i % 2
                        Ob, ROb = Osb2[i % 2], RO2[i % 2]
                        for half in range(2):
                            pov = PS[pp][:, half * 512:half * 512 + 320].rearrange("p (h d) -> p h d", d=80)
                            S.op("vector", lambda e, pov=pov, half=half: e.tensor_tensor(
                                out=dsum[:, 4 * half:4 * half + 4], in0=pov[:, :, 64], in1=esink[:, l, 4 * half:4 * half + 4], op=ALU.add),
                                reads=[Rb[2 * pp + half], Rconst], writes=[Rds])
                        S.op("vector", lambda e: e.reciprocal(out=rinv[:, :], in_=dsum[:, :]), reads=[Rds], writes=[Rds])
                        for half in range(2):
                            pov = PS[pp][:, half * 512:half * 512 + 320].rearrange("p (h d) -> p h d", d=80)
                            S.op("vector", lambda e, pov=pov, half=half: e.tensor_tensor(
                                out=Ob[:, half * 256:(half + 1) * 256].rearrange("p (h d) -> p h d", h=4),
                                in0=pov[:, :, 0:64], in1=bcast_last(rinv[:, 4 * half:4 * half + 4], 64), op=ALU.mult),
                                reads=[Rb[2 * pp + half], Rds], writes=[ROb])

                    def ft(i):
                        Ob, ROb = Osb2[i % 2], RO2[i % 2]
                        pT3 = bank(3).bitcast(BF16)
                        S.group("tensor", [
                            (lambda e, c=c: e.transpose(out=pT3[:, c * 128:(c + 1) * 128], in_=Ob[:, c * 128:(c + 1) * 128], identity=ident[:]))
                            for c in range(4)], reads=[ROb, Rconst], writes=[Rb[3]])
                        S.op("scalar", lambda e: e.copy(out=OT[:, :, i * 128:(i + 1) * 128],
                                                        in_=pT3[:, 0:512].rearrange("p (c t) -> p c t", c=4)),
                             reads=[Rb[3]], writes=[ROT])

                    U = 2 * nt
                    sx(0)
                    for u in range(U):
                        if u + 1 < U:
                            sx(u + 1)
                        pvu(u)
                        if u % 2 == 1:
                            fd(u // 2)
                        elif u >= 2:
                            ft(u // 2 - 1)
                    ft(nt - 1)
                    if MS < 5:
                        for _ in range(5):
                            WS.get(WS.items[WS.next][0])
                            WS.release()
                        return
                    for br_ in range(2):
                        sw_, Rsw = WS.get("B")
                        sg_, Rsg = WS.get("A")
                        wbr = sw_[:, :].rearrange("p (c n) -> p c n", c=4)
                        wg = sg_[:, :].rearrange("p (c n) -> p c n", c=NCH)
                        srcT, Rsrc = (OT, ROT) if br_ == 0 else (mixed, Rmixed)
                        for m in range(NCH):
                            by = c2["y"] % 2
                            c2["y"] += 1
                            bg = 2 + c2["g"] % 2
                            k = c2["g"] % 2
                            c2["g"] += 1
                            S.group("tensor", [
                                (lambda e, c=c, by=by, m=m, wbr=wbr, srcT=srcT: e.matmul(
                                    bank(by)[:, 0:TO], lhsT=wbr[:, c, m * 128:(m + 1) * 128], rhs=srcT[:, c, 0:TO],
                                    start=(c == 0), stop=(c == 3))) for c in range(4)],
                                reads=[Rsw, Rsrc], writes=[Rb[by]])
                            S.group("tensor", [
                                (lambda e, c=c, bg=bg, m=m, wg=wg: e.matmul(
                                    bank(bg)[:, 0:TO], lhsT=wg[:, c, m * 128:(m + 1) * 128], rhs=uT[:, c, 128:128 + TO],
                                    start=(c == 0), stop=(c == NCH - 1))) for c in range(NCH)],
                                reads=[Rsg] + uT_regs(128, 128 + TO), writes=[Rb[bg]])
                            S.op("scalar", lambda e, bg=bg, k=k, m=m, br_=br_: e.activation(
                                out=sig[k][:, 0:TO], in_=bank(bg)[:, 0:TO], func=AF.Sigmoid,
                                bias=bg_all[:, l, 8 * br_ + m:8 * br_ + m + 1]), reads=[Rb[bg], Rconst], writes=[Rsig[k]])
                            if br_ == 0:
                                S.op("vector", lambda e, by=by, k=k, m=m: e.tensor_tensor(
                                    out=za[:, m, 0:TO], in0=sig[k][:, 0:TO], in1=bank(by)[:, 0:TO], op=ALU.mult),
                                    reads=[Rsig[k], Rb[by]], writes=[Rza])
                            else:
                                S.op("vector", lambda e, by=by, k=k: e.tensor_tensor(
                                    out=tg[:, 0:TO], in0=sig[k][:, 0:TO], in1=bank(by)[:, 0:TO], op=ALU.mult),
                                    reads=[Rsig[k], Rb[by]], writes=[Rtg])
                                S.op("vector", lambda e, m=m: e.tensor_tensor(
                                    out=zT[:, m, 0:TO], in0=tg[:, 0:TO], in1=za[:, m, 0:TO], op=ALU.add),
                                    reads=[Rtg, Rza], writes=[RzT])
                        WS.release()
                    so_, Rso = WS.get("A")
                    wo = so_[:, :].rearrange("p (c n) -> p c n", c=NCH)
                    for i in range(nt):
                        ti = m0 + i + 2
                        pd = 2 + c2["d"] % 2
                        c2["d"] += 1
                        S.group("tensor", [
                            (lambda e, pd=pd, half=half, m=m, i=i: e.matmul(
                                PS[pd][:, half * 512:(half + 1) * 512], lhsT=zT[:, m, i * 128:(i + 1) * 128],
                                rhs=wo[:, m, half * 512:(half + 1) * 512], start=(m == 0), stop=(m == NCH - 1)))
                            for half in range(2) for m in range(NCH)],
                            reads=[RzT, Rso], writes=[Rb[2 * pd], Rb[2 * pd + 1]])
                        for half in range(2):
                            S.op("vector", lambda e, pd=pd, half=half, i=i: e.tensor_tensor(
                                out=xblk[:, i, half * 512:(half + 1) * 512], in0=PS[pd][:, half * 512:(half + 1) * 512],
                                in1=xblk[:, i, half * 512:(half + 1) * 512], op=ALU.add),
                                reads=[Rb[2 * pd + half], Rx[i]], writes=[Rx[i]])
                        S.dma("sync", ssem[i], hdst[ti * 128:(ti + 1) * 128, :], xblk[:, i, :], reads=[Rx[i]], writes=[Rhdst[ti]])
                    WS.release()

                for (m0, m1) in split_blocks(lo, hi, MB):
                    do_block(m0, m1)
                S.barrier()

        out_toks = []
        cur = 0
        for pi, (kind, l, which, lo, hi) in enumerate(phases):
            if kind == "ffn":
                final = (pi == 3 * DEPTH - 1)
                toks = ffn_phase(l, which, lo, hi, first=(pi == 0), final=final, hi_=cur)
                out_toks += toks
            else:
                mixer_phase(l, lo, hi, cur)
                cur = 1 - cur
        hres, Rhres = Hb[cur], RH[cur]
        if stop_after is not None and stop_after < 3 * DEPTH:
            with ExitStack() as ps:
                tb = ps.enter_context(nc.sbuf_tensor("dbg", [128, D], F32))
                Rt = Region()
                dsem = newsem()
                for t in range(NH):
                    S.dma("sync", dsem, tb[:], hres[(t + 2) * 128:(t + 3) * 128, :], reads=[Rhres[t + 2]], writes=[Rt])
                    out_toks.append(S.dma("sync", dsem, out_d[t * 128:(t + 1) * 128, :], tb[:], reads=[Rt]))
                S.wait_all("sync", out_toks)
                S.run(block)
                return nc, S
        S.wait_all("sync", out_toks)
        S.run(block)
    return nc, S


def _const_tables(NH, S_len, half):
    NT = NH + 4
    ident = np.eye(128, dtype=np.float32).astype(ml_dtypes.bfloat16)
    rm = np.zeros((64, 64), np.float32)
    for m in range(16):
        k = m + 8 if m < 8 else m - 8
        rm[k, m] = 1.0
    rmat = rm.astype(ml_dtypes.bfloat16)
    j = np.arange(128)[:, None]
    i = np.arange(128)[None, :]
    maskP = (i <= j).astype(np.float32)
    maskN = (i >= j).astype(np.float32)
    zero = np.zeros_like(maskP)
    masks = np.stack([maskP, maskN, maskP if half == 1 else zero, maskN if half == 0 else zero]).astype(ml_dtypes.bfloat16)
    pos = (half * NH * 128 - 256 + np.arange(NT * 128)).astype(np.float32)
    inv_freq = (1.0 / (ROPE_THETA ** (np.arange(0, 16, 2, dtype=np.float32) / np.float32(16)))).astype(np.float32)
    ang = (pos[:, None] * inv_freq[None, :]).astype(np.float32)
    cos = np.cos(ang.astype(np.float64)).astype(np.float32).T
    sin = np.sin(ang.astype(np.float64)).astype(np.float32).T
    ropeC = np.ones((64, NT * 128), np.float32)
    ropeS = np.zeros((64, NT * 128), np.float32)
    ropeC[0:8] = cos
    ropeC[8:16] = cos
    ropeS[0:8] = -sin
    ropeS[8:16] = sin
    pfix = np.zeros((128, 4, 16), np.float32)
    for g in range(4):
        w = 2 << g
        hw = w // 2
        for c in range(8):
            t0 = c
            t1 = S_len - 8 + c
            c0 = min(t0 + hw, S_len) - max(t0 - hw, 0)
            c1 = min(t1 + hw, S_len) - max(t1 - hw, 0)
            pfix[:, g, c] = 1.0 / c0 if half == 0 else 1.0 / w
            pfix[:, g, 8 + c] = 1.0 / c1 if half == 1 else 1.0 / w
    valid = np.ones((128, 2), np.float32)
    valid[:, 0] = 0.0 if half == 0 else 1.0
    valid[:, 1] = 0.0 if half == 1 else 1.0
    return dict(ident=ident, rmat=rmat, masks=masks, ropeC=ropeC, ropeS=ropeS, pfix=pfix, valid=valid)


def make_in_maps(inputs):
    x = np.asarray(inputs["x"], dtype=np.float32)
    Bn, S_len, _ = x.shape
    NH = S_len // 256
    shared = {}
    for n in WSHAPES:
        if n in inputs:
            shared[n] = np.ascontiguousarray(np.asarray(inputs[n], dtype=np.float32))
    bg = np.asarray(inputs["b_gate"], np.float32)
    shared["b_gate_t"] = np.ascontiguousarray(bg.reshape(DEPTH, 16, 128).transpose(0, 2, 1))
    psc = np.asarray(inputs["pool_scale"], np.float32)
    shared["pool_scale_t"] = np.ascontiguousarray(psc.reshape(DEPTH, 4, 128).transpose(0, 2, 1))
    sk = np.asarray(inputs["attn_sink"], np.float32)
    shared["sink_b"] = np.ascontiguousarray(np.broadcast_to(sk[:, None, :], (DEPTH, 128, NHEADS)))
    consts = [_const_tables(NH, S_len, h) for h in range(2)]
    in_maps = []
    for b in range(Bn):
        for half in range(2):
            xp = np.zeros(((NH + 4) * 128, D), np.float32)
            g0 = half * NH * 128 - 256
            lo = max(g0, 0)
            hi = min(g0 + (NH + 4) * 128, S_len)
            xp[lo - g0:hi - g0] = x[b, lo:hi]
            m = dict(shared)
            m.update(consts[half])
            m["x_pad"] = xp
            in_maps.append(m)
    return in_maps, Bn, S_len, NH


_PROG_CACHE = {}


def kernel(**inputs):
    in_maps, Bn, S_len, NH = make_in_maps(inputs)
    if NH not in _PROG_CACHE:
        _PROG_CACHE[NH] = build_program(NH)[0]
    nc = _PROG_CACHE[NH]
    res = run_bass_kernel_spmd(nc, in_maps, core_ids=list(range(len(in_maps))))
    out = np.empty((Bn, S_len, D), np.float32)
    for b in range(Bn):
        for half in range(2):
            out[b, half * NH * 128:(half + 1) * NH * 128] = res.results[b * 2 + half]["out"]
    return out
```
